# Optimizing a Trainium2 kernel written in Bass

```python
import math
import jax, jax.numpy as jnp
from jax import lax
import numpy as np

D_MODEL = 1024
BATCH = 4
SEQ = 4096
DEPTH = 2

N_META = 16
BLOCK = 128
PAD_FRONT = BLOCK - N_META
HEAD_DIM = 64
FOX_HEADS = 8
SWA_Q_HEADS = 8
SWA_KV_HEADS = 2
WINDOW = 128
N_BUCKETS = 32
MAX_DISTANCE = 128
D_FF = 2816
EPS = 1e-6
NEG = -1e30
FORGET_BIAS_INIT = 2.0
FOX_W = FOX_HEADS * HEAD_DIM
SWA_QW = SWA_Q_HEADS * HEAD_DIM
SWA_KVW = SWA_KV_HEADS * HEAD_DIM
D_IN = 3 * FOX_W + FOX_HEADS + SWA_QW + 2 * SWA_KVW + 2 * D_MODEL

kernel_name = "hybrid_fox_swa_sink_macaron_meta"


def rms_norm(x, g):
    xf = x.astype(jnp.float32)
    ms = jnp.mean(xf * xf, axis=-1, keepdims=True)
    return (xf * lax.rsqrt(ms + EPS) * g.astype(jnp.float32)).astype(x.dtype)


def swiglu(h, w_in, w_out):
    gu = h @ w_in
    gate, up = jnp.split(gu, 2, axis=-1)
    return (jax.nn.silu(gate) * up) @ w_out


def t5_bucket(dist):
    n = jnp.maximum(dist, 0)
    max_exact = N_BUCKETS // 2
    nf = jnp.maximum(n, 1).astype(jnp.float32)
    large = max_exact + (jnp.log(nf / max_exact) / math.log(MAX_DISTANCE / max_exact)
                         * (N_BUCKETS - max_exact)).astype(jnp.int32)
    large = jnp.minimum(large, N_BUCKETS - 1)
    return jnp.where(n < max_exact, n, large)


def fox_attention(q, k, v, log_f):
    b, p, h, dh = q.shape
    nb = p // BLOCK
    c_k = jnp.cumsum(log_f, axis=1).transpose(0, 2, 1)
    k_pos = jnp.arange(p)
    k_valid = k_pos >= PAD_FRONT
    qb = q.reshape(b, nb, BLOCK, h, dh).transpose(1, 0, 2, 3, 4)
    cqb = c_k.reshape(b, h, nb, BLOCK).transpose(2, 0, 1, 3)
    scale = dh ** -0.5

    def one_block(args):
        qi, cqi, i = args
        q_pos = i * BLOCK + jnp.arange(BLOCK)
        s = jnp.einsum('bqhd,bkhd->bhqk', qi, k, preferred_element_type=jnp.float32) * scale
        s = s + cqi[..., :, None] - c_k[:, :, None, :]
        mask = (k_pos[None, :] <= q_pos[:, None]) & k_valid[None, :]
        s = jnp.where(mask[None, None], s, NEG)
        pr = jax.nn.softmax(s, axis=-1)
        return jnp.einsum('bhqk,bkhd->bqhd', pr.astype(v.dtype), v)

    out = lax.map(one_block, (qb, cqb, jnp.arange(nb)))
    return out.transpose(1, 0, 2, 3, 4).reshape(b, p, h * dh)


def swa_attention(q, k, v, sinks, rel_bias_table):
    b, p, hq, dh = q.shape
    kv = k.shape[2]
    r = hq // kv
    nb = p // BLOCK
    scale = dh ** -0.5
    qb = q.reshape(b, nb, BLOCK, kv, r, dh)

    def band(a):
        a_ext = jnp.concatenate([jnp.zeros_like(a[:, :BLOCK]), a], axis=1)
        a_ext = a_ext.reshape(b, nb + 1, BLOCK, kv, dh)
        return jnp.concatenate([a_ext[:, :-1], a_ext[:, 1:]], axis=2)

    kb, vb = band(k), band(v)
    km, vm = k[:, PAD_FRONT:BLOCK], v[:, PAD_FRONT:BLOCK]
    q_pos = jnp.arange(nb)[:, None] * BLOCK + jnp.arange(BLOCK)[None, :]
    band_pos = (jnp.arange(nb)[:, None] - 1) * BLOCK + jnp.arange(2 * BLOCK)[None, :]
    meta_pos = PAD_FRONT + jnp.arange(N_META)
    d_band = q_pos[:, :, None] - band_pos[:, None, :]
    m_band = (d_band >= 0) & (d_band < WINDOW) & (band_pos[:, None, :] >= PAD_FRONT)
    d_meta = q_pos[:, :, None] - meta_pos[None, None, :]
    m_meta = d_meta >= WINDOW
    table = rel_bias_table.astype(jnp.float32)
    bias_band = table[t5_bucket(d_band)].transpose(0, 3, 1, 2).reshape(nb, kv, r, BLOCK, 2 * BLOCK)
    bias_meta = table[t5_bucket(d_meta)].transpose(0, 3, 1, 2).reshape(nb, kv, r, BLOCK, N_META)

    s_band = jnp.einsum('bnqgrd,bnkgd->bngrqk', qb, kb, preferred_element_type=jnp.float32) * scale + bias_band[None]
    s_band = jnp.where(m_band[None, :, None, None], s_band, NEG)
    s_meta = jnp.einsum('bnqgrd,bmgd->bngrqm', qb, km, preferred_element_type=jnp.float32) * scale + bias_meta[None]
    s_meta = jnp.where(m_meta[None, :, None, None], s_meta, NEG)
    s = jnp.concatenate([s_band, s_meta], axis=-1)
    sink = sinks.astype(jnp.float32).reshape(kv, r)[None, None, :, :, None, None]
    mx = jnp.maximum(jnp.max(s, axis=-1, keepdims=True), sink)
    pr = jnp.exp(s - mx)
    pr = pr / (jnp.sum(pr, axis=-1, keepdims=True) + jnp.exp(sink - mx))
    pr = pr.astype(v.dtype)
    out = (jnp.einsum('bngrqk,bnkgd->bnqgrd', pr[..., :2 * BLOCK], vb)
           + jnp.einsum('bngrqm,bmgd->bnqgrd', pr[..., 2 * BLOCK:], vm))
    return out.reshape(b, p, hq * dh)


def token_mixer(h, rel_bias_table, w_in, forget_bias, fox_q_norm, fox_k_norm,
                swa_q_norm, swa_k_norm, swa_sinks, w_branch_fox, w_branch_swa, w_out):
    b, p, _ = h.shape
    proj = h @ w_in
    widths = [FOX_W, FOX_W, FOX_W, FOX_HEADS, SWA_QW, SWA_KVW, SWA_KVW, D_MODEL]
    idx = list(np.cumsum(widths))
    qa, ka, va, fa, qb, kb, vb, ga, gb = jnp.split(proj, idx, axis=-1)
    qa = rms_norm(qa.reshape(b, p, FOX_HEADS, HEAD_DIM), fox_q_norm)
    ka = rms_norm(ka.reshape(b, p, FOX_HEADS, HEAD_DIM), fox_k_norm)
    va = va.reshape(b, p, FOX_HEADS, HEAD_DIM)
    log_f = jax.nn.log_sigmoid(fa.astype(jnp.float32) + forget_bias.astype(jnp.float32))
    qb = rms_norm(qb.reshape(b, p, SWA_Q_HEADS, HEAD_DIM), swa_q_norm)
    kb = rms_norm(kb.reshape(b, p, SWA_KV_HEADS, HEAD_DIM), swa_k_norm)
    vb = vb.reshape(b, p, SWA_KV_HEADS, HEAD_DIM)
    o_fox = fox_attention(qa, ka, va, log_f)
    o_swa = swa_attention(qb, kb, vb, swa_sinks, rel_bias_table)
    y = jax.nn.sigmoid(ga) * (o_fox @ w_branch_fox) + jax.nn.sigmoid(gb) * (o_swa @ w_branch_swa)
    return y @ w_out


def setup_inputs(seed: int = 0) -> dict:
    key = jax.random.key(seed)
    ks = jax.random.split(key, 24)
    f32 = jnp.float32

    def nrm(k, shape, scale):
        return jax.random.normal(k, shape, f32) * scale

    def gain(k, shape):
        return 1.0 + 0.1 * jax.random.normal(k, shape, f32)

    return {
        "x": nrm(ks[0], (BATCH, SEQ, D_MODEL), 1.0),
        "meta_tokens": nrm(ks[1], (N_META, D_MODEL), 1.0),
        "rel_bias_table": nrm(ks[2], (N_BUCKETS, SWA_Q_HEADS), 0.5),
        "ffn1_norm": gain(ks[3], (DEPTH, D_MODEL)),
        "ffn1_w_in": nrm(ks[4], (DEPTH, D_MODEL, 2 * D_FF), D_MODEL ** -0.5),
        "ffn1_w_out": nrm(ks[5], (DEPTH, D_FF, D_MODEL), D_FF ** -0.5),
        "mix_norm": gain(ks[6], (DEPTH, D_MODEL)),
        "w_in": nrm(ks[7], (DEPTH, D_MODEL, D_IN), D_MODEL ** -0.5),
        "forget_bias": FORGET_BIAS_INIT + 0.1 * jax.random.normal(ks[8], (DEPTH, FOX_HEADS), f32),
        "fox_q_norm": gain(ks[9], (DEPTH, HEAD_DIM)),
        "fox_k_norm": gain(ks[10], (DEPTH, HEAD_DIM)),
        "swa_q_norm": gain(ks[11], (DEPTH, HEAD_DIM)),
        "swa_k_norm": gain(ks[12], (DEPTH, HEAD_DIM)),
        "swa_sinks": nrm(ks[13], (DEPTH, SWA_Q_HEADS), 0.5),
        "w_branch_fox": nrm(ks[14], (DEPTH, FOX_W, D_MODEL), FOX_W ** -0.5),
        "w_branch_swa": nrm(ks[15], (DEPTH, SWA_QW, D_MODEL), SWA_QW ** -0.5),
        "w_out": nrm(ks[16], (DEPTH, D_MODEL, D_MODEL), D_MODEL ** -0.5),
        "ffn2_norm": gain(ks[17], (DEPTH, D_MODEL)),
        "ffn2_w_in": nrm(ks[18], (DEPTH, D_MODEL, 2 * D_FF), D_MODEL ** -0.5),
        "ffn2_w_out": nrm(ks[19], (DEPTH, D_FF, D_MODEL), D_FF ** -0.5),
    }


def reference(x, meta_tokens, rel_bias_table, ffn1_norm, ffn1_w_in, ffn1_w_out, mix_norm,
              w_in, forget_bias, fox_q_norm, fox_k_norm, swa_q_norm, swa_k_norm, swa_sinks,
              w_branch_fox, w_branch_swa, w_out, ffn2_norm, ffn2_w_in, ffn2_w_out):
    b = x.shape[0]
    pad = jnp.zeros((b, PAD_FRONT, D_MODEL), x.dtype)
    meta = jnp.broadcast_to(meta_tokens.astype(x.dtype)[None], (b, N_META, D_MODEL))
    h = jnp.concatenate([pad, meta, x], axis=1)
    for l in range(DEPTH):
        h = h + 0.5 * swiglu(rms_norm(h, ffn1_norm[l]), ffn1_w_in[l], ffn1_w_out[l])
        h = h + token_mixer(rms_norm(h, mix_norm[l]), rel_bias_table, w_in[l], forget_bias[l],
                            fox_q_norm[l], fox_k_norm[l], swa_q_norm[l], swa_k_norm[l],
                            swa_sinks[l], w_branch_fox[l], w_branch_swa[l], w_out[l])
        h = h + 0.5 * swiglu(rms_norm(h, ffn2_norm[l]), ffn2_w_in[l], ffn2_w_out[l])
    return h[:, BLOCK:]
```

```python
import os
import numpy as np
import concourse.bass as bass
import concourse.mybir as mybir
from concourse.bass_utils import run_bass_kernel_spmd

F32 = mybir.dt.float32
BF16 = mybir.dt.bfloat16
AF = mybir.ActivationFunctionType
ALU = mybir.AluOpType

D = 1024
KC = 8
DFF = 2816
JC = 22
NMETA = 16
NOWN = 2048
T = NMETA + NOWN
GROUPS = [(0, 16), (16, 512), (528, 512), (1040, 512), (1552, 512)]
DEPTH = 2
EPS = 1e-6
D_IN = 4360


class Prod:
    def __init__(self, name, sem):
        self.name, self.sem, self.count = name, sem, 0


class Eng(Prod):
    def __init__(self, name, sem):
        super().__init__(name, sem)
        self.ops = []
        self.waited = {}
        self.pending = False


class Buf:
    __slots__ = ("name", "lw", "rd")

    def __init__(self, name):
        self.name, self.lw, self.rd = name, None, {}


class KB:
    def __init__(self, nc, stack):
        self.nc, self.stack = nc, stack
        self.engs = {}
        for n in ("tensor", "scalar", "vector", "gpsimd", "sync"):
            self.engs[n] = Eng(n, stack.enter_context(nc.semaphore("sem_" + n)))
        self.pe, self.act, self.dve = self.engs["tensor"], self.engs["scalar"], self.engs["vector"]
        self.pool, self.sp = self.engs["gpsimd"], self.engs["sync"]
        self.streams = []
        self.nbuf = 0

    def stream(self, name):
        p = Prod(name, self.stack.enter_context(self.nc.semaphore("st_" + name)))
        self.streams.append(p)
        return p

    def buf(self, name=None):
        self.nbuf += 1
        return Buf(name or f"b{self.nbuf}")

    def _wait(self, eng, prod, idx):
        if idx <= 0:
            return
        if prod is eng and eng is self.pe:
            return
        if eng.waited.get(prod, 0) >= idx:
            return
        assert idx <= prod.count, f"wait on unsignalled {prod.name} {idx}>{prod.count} from {eng.name}"
        eng.ops.append(("wait", prod, idx))
        eng.waited[prod] = idx

    def _deps(self, eng, reads, writes):
        for b in reads:
            if b.lw is not None:
                self._wait(eng, *b.lw)
        for b in writes:
            if b.lw is not None:
                self._wait(eng, *b.lw)
            for p, i in b.rd.items():
                self._wait(eng, p, i)

    def op(self, eng, fn, reads=(), writes=(), sig=True):
        self._deps(eng, reads, writes)
        if eng is not self.pe:
            sig = True
        eng.ops.append(("op", fn, eng if sig else None))
        if sig:
            eng.count += 1
            idx = eng.count
            eng.pending = False
        else:
            idx = eng.count + 1
            eng.pending = True
        for b in reads:
            b.rd[eng] = idx
        for b in writes:
            b.lw = (eng, idx)
            b.rd = {}

    def dma(self, eng, st, fn, reads=(), writes=()):
        self._deps(eng, reads, writes)
        eng.ops.append(("dma", fn, st))
        st.count += 16
        idx = st.count
        for b in reads:
            b.rd[st] = idx
        for b in writes:
            b.lw = (st, idx)
            b.rd = {}

    def barrier(self):
        prods = list(self.engs.values()) + self.streams
        for e in self.engs.values():
            assert not e.pending, e.name
        for e in self.engs.values():
            for p in prods:
                if p is e:
                    continue
                self._wait(e, p, p.count)

    def emit(self):
        nc = self.nc
        with nc.Block() as block:
            for n, e in self.engs.items():
                def body(engine, e=e):
                    for item in e.ops:
                        if item[0] == "wait":
                            engine.wait_ge(item[1].sem, item[2])
                        elif item[0] == "op":
                            ins = item[1](engine)
                            if item[2] is not None:
                                ins.then_inc(item[2].sem, 1)
                        elif item[0] == "cc":
                            item[1](engine).then_inc(item[2].sem)
                        else:
                            item[1](engine).then_inc(item[2].sem, 16)
                getattr(block, n)(body)


CVL = 48
CV_FLAG = 96
CV_MASK = 97
BIG = 30000.0
XW = [8192, 4160, 4418]
X_KSB, X_VSB = 4160, 4288
X_WK, X_TOT = 0, 128
SCALE = 0.125
SKIP = os.environ.get('KSKIP', '')


def build_program(stage):
    from contextlib import ExitStack
    nc = bass.Bass("TRN2", target_bir_lowering=False)
    stack = ExitStack()
    kb = KB(nc, stack)
    pe, act, dve, pool, sp = kb.pe, kb.act, kb.dve, kb.pool, kb.sp

    def din(name, shape):
        return nc.dram_tensor(name, list(shape), F32, kind="ExternalInput").ap()

    x_d = din("x", (NOWN, D))
    meta_d = din("meta", (NMETA, D))
    cmat_d = din("cmat", (128, 384))
    cvec_d = din("cvec", (128, 128))
    swab_d = din("swab", (128, 2 * 8 * 128))
    metab_d = din("metab", (16, 8 * 129))
    metaq_d = din("metaq", (16, 8 * 16))
    f1_in = din("ffn1_w_in", (DEPTH, D, 2 * DFF))
    f1_out = din("ffn1_w_out", (DEPTH, DFF, D))
    f2_in = din("ffn2_w_in", (DEPTH, D, 2 * DFF))
    f2_out = din("ffn2_w_out", (DEPTH, DFF, D))
    win_d = din("w_in", (DEPTH, D, D_IN))
    wbf_d = din("w_branch_fox", (DEPTH, 512, D))
    wbs_d = din("w_branch_swa", (DEPTH, 512, D))
    wout_d = din("w_out", (DEPTH, D, D))
    wvsf_d = din("w_vsf", (DEPTH, D, 136))
    out_d = nc.dram_tensor("out", [NOWN, D], F32, kind="ExternalOutput").ap()
    xin_d = [[nc.dram_tensor(f"xin{l}_{c}", [128, XW[c]], BF16).ap() for c in range(3)]
             + [nc.dram_tensor(f"xin{l}_3", [128, 256], F32).ap()] for l in range(DEPTH)]
    xout_d = [[nc.dram_tensor(f"xout{l}_{c}", [256, XW[c]], BF16).ap() for c in range(3)]
              + [nc.dram_tensor(f"xout{l}_3", [256, 256], F32).ap()] for l in range(DEPTH)]

    def sb(name, shape, dt):
        return stack.enter_context(nc.sbuf_tensor("s_" + name, list(shape), dt))

    def ps(name, shape=(128, 512), dt=F32):
        return stack.enter_context(nc.psum_tensor(name, list(shape), dt))

    B = kb.buf
    hT = sb("hT", (128, KC, T), F32)
    hT_b = [[B(f"hT{k}_{g}") for g in range(5)] for k in range(KC)]
    cmat = sb("cmat", (128, 384), F32)
    cmat_b = B("cmat")
    ident = cmat[:, 0:128]
    U_f = cmat[:, 128:256]
    cbf = sb("cbf", (128, 384), BF16)
    cbf_b = B("cbf")
    ident_bf, tri_bf, bd_bf = cbf[:, 0:128], cbf[:, 128:256], cbf[:, 256:384]
    cvec = sb("cvec", (128, 128), F32)
    cvec_b = B("cvec")
    ones_bf = sb("ones_bf", (128, 128), BF16)
    ones_f = sb("ones_f", (128, 128), F32)
    ones_b = B("ones")
    swab = sb("swab", (128, 2, 8, 128), F32)
    metab = sb("metab", (16, 8, 129), F32)
    metaq = sb("metaq", (16, 8, 16), F32)
    esink = sb("esink", (128, 8), F32)
    bias_b = B("biasconst")
    esink_b = B("esink")
    hn = sb("hn", (128, KC, 512), BF16)
    hn_b = B("hn")
    hn_g, hn_gb = hn, hn_b
    lnv = sb("lnv", (128, 512), F32)
    lnv_b = B("lnv")
    rstd = sb("rstd", (128, 512), F32)
    rstd_b = B("rstd")
    NH = 6
    wt_all = sb("wt", (128, NH * 2048), BF16)
    wt_b = [B(f"wt{i}") for i in range(NH)]
    wt_st = [kb.stream(f"wt{i}") for i in range(NH)]
    cst = kb.stream("const")

    AR = 44520
    arena = sb("arena", (128, AR), BF16)
    aoff = [0]

    def carve(n, dt=BF16):
        n16 = n if dt == BF16 else 2 * n
        o = aoff[0]
        aoff[0] += n16 + (n16 % 2)
        assert aoff[0] <= AR, aoff[0]
        v = arena[:, o:o + n16]
        return v if dt == BF16 else v.bitcast(F32)

    aoff[0] = 0
    PW = 1040
    aT = carve(JC * PW).rearrange("p (j n) -> p j n", j=JC)
    aT_b = [[B(f"aT{j}_{gi}") for gi in range(3)] for j in range(JC)]
    hnF = carve(KC * PW).rearrange("p (k n) -> p k n", k=KC)
    hnF_b = [B(f"hnF{gi}") for gi in range(3)]
    wo = [carve(JC * 256) for _ in range(2)]
    wo_b = [B(f"wo{i}") for i in range(2)]
    wo_st = [kb.stream(f"wo{i}") for i in range(2)]
    sg = [carve(512, F32) for _ in range(2)]
    sg_b = [B(f"sg{i}") for i in range(2)]
    stg = [wo[i][:, 0:2 * D].bitcast(F32) for i in range(2)]
    stg_b = wo_b
    stg_st = [kb.stream(f"stg{i}") for i in range(2)]
    aoff[0] = 0
    KT = carve(4 * T).rearrange("p (c t) -> p c t", c=4)
    KT_b = B("KT")
    VX = carve(8 * 17 * 65).rearrange("p (h b e) -> p h b e", h=8, b=17)
    VX_b = B("VX")
    KS = carve(T)
    KS_b = B("KS")
    VSX = carve(2 * 17 * 65).rearrange("p (h b e) -> p h b e", h=2, b=17)
    VSX_b = B("VSX")
    KSb = carve(128)
    VSb = carve(130).rearrange("p (h e) -> p h e", h=2)
    bnd_b = B("bnd")
    pK = [carve(2048) for _ in range(2)]
    pV = [carve(1040).rearrange("p (b e) -> p b e", b=16) for _ in range(2)]
    pKV_b = [B(f"pKV{i}") for i in range(2)]
    pKV_st = [kb.stream(f"pKV{i}") for i in range(2)]
    QFp = carve(8 * 512).rearrange("p (h n) -> p h n", h=8)
    QF_b = B("QFp")
    QSO = carve(8 * 512)
    QS = QSO[:, 0:2048].rearrange("p (c n) -> p c n", c=4)
    otf = QSO[:, 2048:4096].rearrange("p (q f) -> p q f", q=4)
    yT = QSO.rearrange("p (c n) -> p c n", c=8)
    QS_b, otf_b = B("QS"), B("otf")
    YT_bufs = [QS_b, otf_b]
    PT = [carve(512) for _ in range(3)]
    PT_b = [B(f"PT{i}") for i in range(3)]
    sqh, sqh_b = PT[2], PT_b[2]
    sT = [carve(512, F32) for _ in range(2)]
    sT_b = [B(f"sT{i}") for i in range(2)]
    t1, t1_b = lnv, lnv_b
    ots = carve(4 * 512).rearrange("p (q f) -> p q f", q=4)
    ots_b = B("ots")
    oTf = pK[0].rearrange("p (c n) -> p c n", c=4)
    oTs = pK[1].rearrange("p (c n) -> p c n", c=4)
    oTf_b, oTs_b = pKV_b[0], pKV_b[1]
    lf_all = carve(17 * 8, F32).rearrange("p (b h) -> p b h", b=17)
    lf_b = B("lf")
    zt = carve(8, F32)
    et = carve(8, F32)
    zt_b, et_b = B("zt"), B("et")
    wk_all = carve(33 * 8, F32).rearrange("p (b h) -> p b h", b=33)
    tot_all = carve(33 * 8, F32).rearrange("p (b h) -> p b h", b=33)
    E_all = carve(33 * 8, F32).rearrange("p (b h) -> p b h", b=33)
    WE = carve(33 * 8, F32).rearrange("p (b h) -> p b h", b=33)
    tmp8 = carve(8, F32)
    wk_b, tot_b, E_b, WE_b, tmp8_b = B("wk"), B("tot"), B("E"), B("WE"), B("tmp8")
    wkp_b, totp_b = B("wkp"), B("totp")
    bias_g, biasg_b = tot_all, tot_b
    den = carve(8, F32)
    rcp = carve(8, F32)
    den_b, rcp_b = B("den"), B("rcp")
    xst = [[kb.stream(f"xst{l}_{c}") for c in range(4)] for l in range(DEPTH)]
    ccs = [[kb.stream(f"cc{l}_{c}") for c in range(4)] for l in range(DEPTH)]
    xld = kb.stream("xld")
    xd_b = [[B(f"xd{l}_{c}") for c in range(4)] for l in range(DEPTH)]

    pbank = [ps(f"pb{i}") for i in range(8)]
    pbank_b = [B(f"pb{i}") for i in range(8)]
    pbank_bf = [p.bitcast(BF16) for p in pbank]

    kb.dma(sp, cst, lambda e: e.dma_start(out=cmat[:], in_=cmat_d), writes=[cmat_b])
    kb.dma(sp, cst, lambda e: e.dma_start(out=cvec[:], in_=cvec_d), writes=[cvec_b])
    kb.dma(sp, cst, lambda e: e.dma_start(out=swab[:].rearrange("p a h q -> p (a h q)"), in_=swab_d),
           writes=[bias_b])
    kb.dma(sp, cst, lambda e: e.dma_start(out=metab[:].rearrange("p h q -> p (h q)"), in_=metab_d),
           writes=[bias_b])
    kb.dma(sp, cst, lambda e: e.dma_start(out=metaq[:].rearrange("p h q -> p (h q)"), in_=metaq_d),
           writes=[bias_b])
    for b_ in (cmat_b, cvec_b, bias_b):
        b_.lw = (cst, cst.count)
    kb.op(dve, lambda e: e.memset(ones_bf[:], 1.0), writes=[ones_b])
    kb.op(dve, lambda e: e.memset(ones_f[:], 1.0), writes=[ones_b])
    kb.op(dve, lambda e: e.tensor_copy(out=cbf[:], in_=cmat[:]), reads=[cmat_b], writes=[cbf_b])

    def load_block(src_ap, nrows, col0, slot):
        kb.dma(sp, stg_st[slot], lambda e: e.dma_start(out=stg[slot][:nrows, :], in_=src_ap),
               writes=[stg_b[slot]])
        g = [i for i, (c0, n) in enumerate(GROUPS) if c0 <= col0 < c0 + n][0]
        for q in range(2):
            pb = q
            for kk in range(4):
                k = q * 4 + kk
                kb.op(pe, lambda e, k=k, kk=kk, pb=pb: e.transpose(
                    out=pbank[pb][:, kk * nrows:(kk + 1) * nrows],
                    in_=stg[slot][:nrows, k * 128:(k + 1) * 128],
                    identity=ident[:nrows, :nrows]),
                    reads=[stg_b[slot], cmat_b], writes=[pbank_b[pb]], sig=(kk == 3))
            if q == 0:
                fn = lambda e, q=q, pb=pb: e.activation(
                    out=hT[:, q * 4:(q + 1) * 4, col0:col0 + nrows],
                    in_=pbank[pb][:, :4 * nrows].rearrange("p (a n) -> p a n", a=4), func=AF.Copy)
            else:
                fn = lambda e, q=q, pb=pb: e.tensor_copy(
                    out=hT[:, q * 4:(q + 1) * 4, col0:col0 + nrows],
                    in_=pbank[pb][:, :4 * nrows].rearrange("p (a n) -> p a n", a=4))
            kb.op(act if q == 0 else dve, fn, reads=[pbank_b[pb]],
                  writes=[hT_b[k][g] for k in range(q * 4, q * 4 + 4)])

    def load_x(blks):
        for blk in blks:
            load_block(x_d[blk * 128:(blk + 1) * 128, :], 128, NMETA + blk * 128, (blk + 1) % 2)

    load_block(meta_d, NMETA, 0, 0)
    load_x(range(8))

    wt_rr = [0]

    class WB:
        pass

    def load_w(parts, kdim, cw):
        nh = 1 if kdim * cw <= 2048 else 2
        i = wt_rr[0] % NH
        if nh == 2 and i % 2 == 1:
            i = (i + 1) % NH
            wt_rr[0] += 1
        wt_rr[0] += nh
        bufs = [wt_b[i + d_] for d_ in range(nh)]
        st_ = wt_st[i]
        view = wt_all[:, i * 2048:i * 2048 + kdim * cw].rearrange("p (k n) -> p k n", k=kdim)
        for n_, (src3, coff, w_) in enumerate(parts):
            kb.dma(pool, st_, lambda e, src3=src3, coff=coff, w_=w_: e.dma_start(
                out=view[:, :src3.shape[1], coff:coff + w_], in_=src3),
                writes=bufs if n_ == 0 else [])
        for b_ in bufs:
            b_.lw = (st_, st_.count)
        return view, bufs

    wo_rr = [0]

    def load_wo(src3, kdim, cw):
        i = wo_rr[0] % 2
        wo_rr[0] += 1
        view = wo[i][:, :kdim * cw].rearrange("p (k n) -> p k n", k=kdim)
        kb.dma(pool, wo_st[i], lambda e: e.dma_start(out=view, in_=src3), writes=[wo_b[i]])
        return view, wo_b[i]

    def rmsnorm(g, gcol, hn=None, hn_b=None):
        if hn is None:
            hn, hn_b = hn_g, hn_gb
        c0, n = GROUPS[g]
        kb.op(act, lambda e: e.activation(out=hn[:, :, :n], in_=hT[:, :, c0:c0 + n], func=AF.Square),
              reads=[hT_b[k][g] for k in range(KC)], writes=[hn_b])
        pb = 2
        for k in range(KC):
            kb.op(pe, lambda e, k=k: e.matmul(out=pbank[pb][:, :n], lhsT=ones_bf[:], rhs=hn[:, k, :n],
                                              start=(k == 0), stop=(k == KC - 1)),
                  reads=[hn_b, ones_b], writes=[pbank_b[pb]], sig=(k == KC - 1))
        kb.op(act, lambda e: e.activation(out=lnv[:, :n], in_=pbank[pb][:, :n], func=AF.Ln,
                                          scale=1.0 / D, bias=EPS),
              reads=[pbank_b[pb]], writes=[lnv_b])
        kb.op(act, lambda e: e.activation(out=rstd[:, :n], in_=lnv[:, :n], func=AF.Exp, scale=-0.5),
              reads=[lnv_b], writes=[rstd_b])
        for k in range(KC):
            kb.op(dve, lambda e, k=k: e.scalar_tensor_tensor(
                out=hn[:, k, :n], in0=hT[:, k, c0:c0 + n], scalar=cvec[:, gcol + k:gcol + k + 1],
                in1=rstd[:, :n], op0=ALU.mult, op1=ALU.mult),
                reads=[hT_b[k][g], cvec_b, rstd_b], writes=[hn_b])

    def headnorm(pb, n, gcol, out_ap, out_bufs, halves=None):
        kb.op(act, lambda e: e.activation(out=sqh[:, :n], in_=pbank[pb][:, :n], func=AF.Square),
              reads=[pbank_b[pb]], writes=[sqh_b])
        kb.op(pe, lambda e: e.matmul(out=pbank[7][:, :n], lhsT=bd_bf, rhs=sqh[:, :n], start=True, stop=True),
              reads=[sqh_b, cbf_b], writes=[pbank_b[7]])
        kb.op(act, lambda e: e.activation(out=lnv[:, :n], in_=pbank[7][:, :n], func=AF.Ln,
                                          scale=1.0 / 64, bias=EPS),
              reads=[pbank_b[7]], writes=[lnv_b])
        kb.op(act, lambda e: e.activation(out=rstd[:, :n], in_=lnv[:, :n], func=AF.Exp, scale=-0.5),
              reads=[lnv_b], writes=[rstd_b])
        if halves is None:
            kb.op(dve, lambda e: e.scalar_tensor_tensor(
                out=out_ap, in0=pbank[pb][:, :n], scalar=cvec[:, gcol:gcol + 1], in1=rstd[:, :n],
                op0=ALU.mult, op1=ALU.mult),
                reads=[pbank_b[pb], cvec_b, rstd_b], writes=out_bufs)
        else:
            for hi, oap in enumerate(halves):
                ps_ = slice(hi * 64, (hi + 1) * 64)
                kb.op(dve, lambda e, oap=oap, ps_=ps_: e.scalar_tensor_tensor(
                    out=oap, in0=pbank[pb][ps_, :n], scalar=cvec[ps_, gcol:gcol + 1], in1=rstd[ps_, :n],
                    op0=ALU.mult, op1=ALU.mult),
                    reads=[pbank_b[pb], cvec_b, rstd_b], writes=out_bufs)

    def chain(items, banks=(3, 4, 5, 6)):
        prev = None
        for i, (pf_, nf_) in enumerate(items):
            pb = banks[i % len(banks)]
            pf_(pb)
            if prev is not None:
                prev()
            prev = (lambda pb=pb, nf_=nf_: nf_(pb))
        if prev is not None:
            prev()

    def proj_fm(wv, wv_b, kdim, mcol, rhs_fn, rhs_bufs, pb, n):
        for k in range(kdim):
            kb.op(pe, lambda e, k=k: e.matmul(out=pbank[pb][:, :n], lhsT=wv[:, k, mcol:mcol + 128],
                                              rhs=rhs_fn(k), start=(k == 0), stop=(k == kdim - 1)),
                  reads=list(wv_b) + rhs_bufs, writes=[pbank_b[pb]], sig=(k == kdim - 1))

    def ffn_views(groups):
        offs = []
        o = 0
        for g in groups:
            offs.append(o)
            o += GROUPS[g][1]
        return offs, [hnF[:, :, offs[gi]:offs[gi] + GROUPS[g][1]] for gi, g in enumerate(groups)]

    def ffn_norms(groups, gcol):
        _, views = ffn_views(groups)
        for gi, g in enumerate(groups):
            rmsnorm(g, gcol, views[gi], hnF_b[gi])

    def ffn(groups, gcol, w_in_l, w_out_l, pre_normed=False, hoist=None, hoist0=None):
        offs, views = ffn_views(groups)
        w_in3 = w_in_l.rearrange("(k p) n -> p k n", p=128)
        tiles = [(i * 256, 256) for i in range(11)]
        itc = [0]

        def gate_up(wg, wg_b, wu, wu_b, h0, jj, gi):
            g = groups[gi]
            j = h0 // 128 + jj
            n = GROUPS[g][1]
            v = views[gi]
            it = itc[0]
            itc[0] += 1
            pg, pu = 3 + 2 * (it % 2), 4 + 2 * (it % 2)
            s_ = it % 2
            proj_fm(wg, wg_b, KC, jj * 128, lambda k: v[:, k, :], [hnF_b[gi]], pg, n)
            proj_fm(wu, wu_b, KC, jj * 128, lambda k: v[:, k, :], [hnF_b[gi]], pu, n)
            kb.op(act, lambda e: e.activation(out=sg[s_][:, :n], in_=pbank[pg][:, :n], func=AF.Silu),
                  reads=[pbank_b[pg]], writes=[sg_b[s_]])
            kb.op(dve, lambda e: e.tensor_tensor(
                out=aT[:, j, offs[gi]:offs[gi] + n], in0=pbank[pu][:, :n], in1=sg[s_][:, :n], op=ALU.mult),
                reads=[pbank_b[pu], sg_b[s_]], writes=[aT_b[j][gi]])

        for ti_, (h0, hw) in enumerate(tiles):
            ng = len(groups)
            if ti_ == 0 and not pre_normed:
                rmsnorm(groups[0], gcol, views[0], hnF_b[0])
                if ng > 1:
                    rmsnorm(groups[1], gcol, views[1], hnF_b[1])
            wg, wg_b = load_w([(w_in3[:, :, h0:h0 + hw], 0, hw)], KC, hw)
            wu, wu_b = load_w([(w_in3[:, :, DFF + h0:DFF + h0 + hw], 0, hw)], KC, hw)
            if ti_ == 0:
                for gi in range(ng):
                    for jj in range(hw // 128):
                        gate_up(wg, wg_b, wu, wu_b, h0, jj, gi)
                    if gi + 2 < ng and not pre_normed:
                        rmsnorm(groups[gi + 2], gcol, views[gi + 2], hnF_b[gi + 2])
            else:
                for jj in range(hw // 128):
                    for gi in range(len(groups)):
                        gate_up(wg, wg_b, wu, wu_b, h0, jj, gi)
            if ti_ == 0 and hoist0 is not None:
                hoist0()
        w_out3 = w_out_l.rearrange("(j p) n -> p j n", p=128)
        it = 0
        for mt in range(4):
            wv, wv_b = load_wo(w_out3[:, :, mt * 256:(mt + 1) * 256], JC, 256)
            if mt == 0 and hoist is not None:
                hoist()
            for mm in range(2):
                m = mt * 2 + mm
                for gi, g in enumerate(groups):
                    c0, n = GROUPS[g]
                    o_ = offs[gi]
                    pb = it % 2
                    it += 1
                    for j in range(JC):
                        kb.op(pe, lambda e, j=j, mm=mm, pb=pb, wv=wv, n=n, o_=o_: e.matmul(
                            out=pbank[pb][:, :n], lhsT=wv[:, j, mm * 128:(mm + 1) * 128], rhs=aT[:, j, o_:o_ + n],
                            start=(j == 0), stop=(j == JC - 1)),
                            reads=[wv_b, aT_b[j][gi]], writes=[pbank_b[pb]], sig=(j == JC - 1))
                    kb.op(dve, lambda e, m=m, pb=pb, c0=c0, n=n: e.scalar_tensor_tensor(
                        out=hT[:, m, c0:c0 + n], in0=pbank[pb][:, :n], scalar=0.5, in1=hT[:, m, c0:c0 + n],
                        op0=ALU.mult, op1=ALU.add),
                        reads=[pbank_b[pb], hT_b[m][g]], writes=[hT_b[m][g]])

    def mixer_kv(l):
        cb = CVL * l
        w3 = win_d[l].rearrange("(k p) n -> p k n", p=128)
        kb.op(dve, lambda e: e.memset(VX[:, :, :, 64:65], 1.0), writes=[VX_b])
        kb.op(dve, lambda e: e.memset(VX[:, :, 0, :], 0.0), writes=[VX_b])
        kb.op(dve, lambda e: e.memset(VX[:16, :, 0, 64:65], 1.0), writes=[VX_b])
        kb.op(dve, lambda e: e.memset(VSX[:, :, :, 64:65], 1.0), writes=[VSX_b])
        kb.op(dve, lambda e: e.memset(lf_all[:, 0, :], 0.0), writes=[lf_b])
        kb.op(act, lambda e: e.activation(out=esink[:], in_=cvec[:, cb + 36:cb + 44], func=AF.Exp),
              reads=[cvec_b], writes=[esink_b])
        hn_alt = QFp.rearrange("p h n -> p (h n)").rearrange("p (k n) -> p k n", k=KC)
        hbufs = [(hn_g, hn_gb), (hn_alt, QF_b)]

        def kv_group(g):
            c0, n = GROUPS[g]
            hn, hn_b = hbufs[g % 2]
            if g == 0:
                rmsnorm(g, cb + 8, hn, hn_b)
            wk_, wkb_ = load_w([(w3[:, :, 512:1024], 0, 512)], KC, 512)
            wks_, wksb_ = load_w([(w3[:, :, 2056:2184], 0, 128)], KC, 128)
            items = []
            for mc in range(4):
                items.append((
                    lambda pb, mc=mc: proj_fm(wk_, wkb_, KC, mc * 128, lambda k: hn[:, k, :n], [hn_b], pb, n),
                    lambda pb, mc=mc: headnorm(pb, n, cb + 25, KT[:, mc, c0:c0 + n], [KT_b])))
            items.append((
                lambda pb: proj_fm(wks_, wksb_, KC, 0, lambda k: hn[:, k, :n], [hn_b], pb, n),
                lambda pb: headnorm(pb, n, cb + 27, KS[:, c0:c0 + n], [KS_b])))
            chain(items)
            if g + 1 < 5:
                rmsnorm(g + 1, cb + 8, *hbufs[(g + 1) % 2])
            wv_, wvb_ = load_w([(w3[:, :, 1024:1536], 0, 512)], KC, 512)
            if "f" in SKIP:
                return
            wf_, wfb_ = load_w([(wvsf_d[l].rearrange("(k p) n -> p k n", p=128), 0, 136)], KC, 136)
            nblk = max(1, n // 128)
            for bi in range(nblk):
                rows = min(n, 128)
                lb = 0 if g == 0 else 4 * (g - 1) + 1 + bi
                cs = bi * 128
                pv, pf = bi % 2, 2
                for k in range(KC):
                    kb.op(pe, lambda e, k=k, pv=pv, cs=cs, rows=rows: e.matmul(
                        out=pbank[pv][:rows, :512], lhsT=hn[:, k, cs:cs + rows], rhs=wv_[:, k, :],
                        start=(k == 0), stop=(k == KC - 1)),
                        reads=[hn_b] + wvb_, writes=[pbank_b[pv]], sig=(k == KC - 1))
                if "f" in SKIP:
                    continue
                kb.op(act, lambda e, pv=pv, rows=rows, lb=lb: e.activation(
                    out=VX[:rows, :, lb, 0:64], in_=pbank[pv][:rows, :512].rearrange("p (h d) -> p h d", h=8),
                    func=AF.Copy), reads=[pbank_b[pv]], writes=[VX_b])
                for k in range(KC):
                    kb.op(pe, lambda e, k=k, pf=pf, cs=cs, rows=rows: e.matmul(
                        out=pbank[pf][:rows, :136], lhsT=hn[:, k, cs:cs + rows], rhs=wf_[:, k, :],
                        start=(k == 0), stop=(k == KC - 1)),
                        reads=[hn_b] + wfb_, writes=[pbank_b[pf]], sig=(k == KC - 1))
                kb.op(act, lambda e, pf=pf, rows=rows, lb=lb: e.activation(
                    out=VSX[:rows, :, lb, 0:64], in_=pbank[pf][:rows, :128].rearrange("p (h d) -> p h d", h=2),
                    func=AF.Copy), reads=[pbank_b[pf]], writes=[VSX_b])
                if "a" in SKIP:
                    continue
                kb.op(act, lambda e, pf=pf, rows=rows: e.activation(
                    out=zt[:rows, :], in_=pbank[pf][:rows, 128:136], func=AF.Copy),
                    reads=[pbank_b[pf]], writes=[zt_b])
                kb.op(dve, lambda e, rows=rows: e.tensor_tensor(
                    out=et[:rows, :], in0=zt[:rows, :], in1=cvec[:rows, cb + 28:cb + 36], op=ALU.add),
                    reads=[zt_b, cvec_b], writes=[et_b])
                kb.op(act, lambda e, rows=rows: e.activation(out=zt[:rows, :], in_=et[:rows, :], func=AF.Exp,
                                                             scale=-1.0),
                      reads=[et_b], writes=[zt_b])
                kb.op(dve, lambda e, rows=rows: e.tensor_scalar(out=et[:rows, :], in0=zt[:rows, :], scalar1=1.0,
                                                                scalar2=None, op0=ALU.add),
                      reads=[zt_b], writes=[et_b])
                kb.op(act, lambda e, rows=rows, lb=lb: e.activation(out=lf_all[:rows, lb, :], in_=et[:rows, :],
                                                                    func=AF.Ln),
                      reads=[et_b], writes=[lf_b])
        for g in range(5):
            kv_group(g)
        kb.op(dve, lambda e: e.memset(QFp[:, :, :], 0.0), writes=[QF_b])
        if "c" in SKIP:
            return
        lf2 = lf_all[:].rearrange("p b h -> p (b h)") if False else lf_all.rearrange("p b h -> p (b h)")
        kb.op(pe, lambda e: e.matmul(out=pbank[0][:, :136], lhsT=U_f, rhs=lf2, start=True, stop=True),
              reads=[lf_b, cmat_b], writes=[pbank_b[0]])
        kb.op(pe, lambda e: e.matmul(out=pbank[1][:, :136], lhsT=ones_f[:], rhs=lf2, start=True, stop=True),
              reads=[lf_b, ones_b], writes=[pbank_b[1]])
        wk2 = wk_all.rearrange("p b h -> p (b h)")
        tot2 = tot_all.rearrange("p b h -> p (b h)")
        kb.op(dve, lambda e: e.tensor_copy(out=wk2[:, 0:8], in_=pbank[0][:, 0:8]),
              reads=[pbank_b[0]], writes=[wk_b])
        kb.op(dve, lambda e: e.tensor_copy(out=wk2[:, 136:264], in_=pbank[0][:, 8:136]),
              reads=[pbank_b[0]], writes=[wk_b])
        kb.op(act, lambda e: e.activation(out=tot2[:, 0:8], in_=pbank[1][:, 0:8], func=AF.Copy),
              reads=[pbank_b[1]], writes=[tot_b])
        kb.op(act, lambda e: e.activation(out=tot2[:, 136:264], in_=pbank[1][:, 8:136], func=AF.Copy),
              reads=[pbank_b[1]], writes=[tot_b])

    def exchange(l):
        xi = xin_d[l]
        def st(c, fn, reads):
            kb.dma(sp, xst[l][c], fn, reads=reads, writes=[])
        st(0, lambda e: e.dma_start(out=xi[0][:, :].rearrange("p (c t) -> p c t", c=4), in_=KT[:, :, NMETA:T]),
           [KT_b])
        st(1, lambda e: e.dma_start(out=xi[1][:, :].rearrange("p (h x) -> p h x", h=4),
                                    in_=VX[:, 0:4, 1:17, :].rearrange("p h b e -> p h (b e)")), [VX_b])
        st(2, lambda e: e.dma_start(out=xi[2][:, 0:4160].rearrange("p (h x) -> p h x", h=4),
                                    in_=VX[:, 4:8, 1:17, :].rearrange("p h b e -> p h (b e)")), [VX_b])
        st(3, lambda e: e.dma_start(
            out=xi[3][:, X_WK:X_WK + 128], in_=wk_all[:, 17:33, :].rearrange("p b h -> p (b h)")), [wk_b])
        st(3, lambda e: e.dma_start(
            out=xi[3][:, X_TOT:X_TOT + 128], in_=tot_all[:, 17:33, :].rearrange("p b h -> p (b h)")), [tot_b])
        st(2, lambda e: e.dma_start(out=xi[2][:, X_KSB:X_KSB + 128], in_=KS[:, T - 128:T]), [KS_b])
        st(2, lambda e: e.dma_start(
            out=xi[2][:, X_VSB:X_VSB + 130].rearrange("p (h e) -> p h e", h=2), in_=VSX[:, :, 16, :]), [VSX_b])
        for c in (3, 0, 1, 2):
            xd_b[l][c].lw = (xst[l][c], xst[l][c].count)

    def exchange_cc(l):
        xi = xin_d[l]
        for c in (3, 0, 1, 2):
            kb._deps(pool, [xd_b[l][c]], [])
            pool.ops.append(("cc", lambda e, c=c: e.collective_compute(
                "AllGather", ALU.bypass, replica_groups=[[0, 1], [2, 3], [4, 5], [6, 7]],
                ins=[xi[c]], outs=[xout_d[l][c]]), ccs[l][c]))
            ccs[l][c].count += 1
            xd_b[l][c].lw = (ccs[l][c], 1)
        xo = xout_d[l][2]
        xo3 = xout_d[l][3]
        kb.dma(sp, xld, lambda e: e.dma_start(
            out=wk_all[:, 1:17, :].rearrange("p b h -> p (b h)"), in_=xo3[0:128, X_WK:X_WK + 128]),
            reads=[xd_b[l][3]], writes=[wkp_b])
        kb.dma(sp, xld, lambda e: e.dma_start(
            out=tot_all[:, 1:17, :].rearrange("p b h -> p (b h)"), in_=xo3[0:128, X_TOT:X_TOT + 128]),
            reads=[xd_b[l][3]], writes=[totp_b])
        kb.dma(sp, xld, lambda e: e.dma_start(out=KSb, in_=xo[0:128, X_KSB:X_KSB + 128]),
               reads=[xd_b[l][2]], writes=[bnd_b])
        kb.dma(sp, xld, lambda e: e.dma_start(
            out=VSb, in_=xo[0:128, X_VSB:X_VSB + 130].rearrange("p (h e) -> p h e", h=2)),
            reads=[xd_b[l][2]], writes=[bnd_b])
        for b_ in (wkp_b, totp_b, bnd_b):
            b_.lw = (xld, xld.count)

    def exchange_E(l):
        TT = ALU
        kb.op(dve, lambda e: e.memset(E_all[:, 0, :], 0.0), writes=[E_b])
        kb.op(dve, lambda e: e.tensor_copy(out=E_all[:, 1, :], in_=tot_all[:, 0, :]), reads=[tot_b], writes=[E_b])
        for j in range(2, 17):
            kb.op(dve, lambda e, j=j: e.tensor_tensor(out=E_all[:, j, :], in0=E_all[:, j - 1, :],
                                                      in1=tot_all[:, j - 1, :], op=TT.add),
                  reads=[E_b, tot_b, totp_b], writes=[E_b])
        kb.op(dve, lambda e: e.tensor_tensor(out=tmp8[:, :], in0=E_all[:, 16, :], in1=tot_all[:, 16, :], op=TT.add),
              reads=[E_b, tot_b, totp_b], writes=[tmp8_b])
        kb.op(dve, lambda e: e.tensor_tensor(out=tmp8[:, :], in0=tmp8[:, :], in1=tot_all[:, 0, :], op=TT.subtract),
              reads=[tmp8_b, tot_b], writes=[tmp8_b])
        kb.op(dve, lambda e: e.scalar_tensor_tensor(
            out=E_all[:, 17, :], in0=tmp8[:, :], scalar=cvec[:, CV_FLAG:CV_FLAG + 1], in1=tot_all[:, 0, :],
            op0=TT.mult, op1=TT.add), reads=[tmp8_b, cvec_b, tot_b], writes=[E_b])
        for j in range(18, 33):
            kb.op(dve, lambda e, j=j: e.tensor_tensor(out=E_all[:, j, :], in0=E_all[:, j - 1, :],
                                                      in1=tot_all[:, j - 1, :], op=TT.add),
                  reads=[E_b, tot_b, totp_b], writes=[E_b])
        kb.op(dve, lambda e: e.tensor_scalar(
            out=E_all[:, 1:17, :], in0=E_all[:, 1:17, :], scalar1=cvec[:, CV_MASK:CV_MASK + 1], scalar2=None,
            op0=TT.add), reads=[E_b, cvec_b], writes=[E_b])
        kb.op(dve, lambda e: e.tensor_tensor(out=WE[:], in0=wk_all[:], in1=E_all[:], op=TT.add),
              reads=[wk_b, wkp_b, E_b], writes=[WE_b])

    def mixer_pre(l, g):
        cb = CVL * l
        c0, n = GROUPS[g]
        w3 = win_d[l].rearrange("(k p) n -> p k n", p=128)
        rmsnorm(g, cb + 8)
        wq_, wqb_ = load_w([(w3[:, :, 0:512], 0, 512)], KC, 512)
        wq2_, wq2b_ = load_w([(w3[:, :, 1544:2056], 0, 512)], KC, 512)
        items = []
        for mc in range(4):
            items.append((
                lambda pb, mc=mc: proj_fm(wq_, wqb_, KC, mc * 128, lambda k: hn[:, k, :n], [hn_b], pb, n),
                lambda pb, mc=mc: headnorm(pb, n, cb + 24, None, [QF_b],
                                           halves=[QFp[0:64, 2 * mc, :n], QFp[64:128, 2 * mc + 1, :n]])))
        for mc in range(4):
            items.append((
                lambda pb, mc=mc: proj_fm(wq2_, wq2b_, KC, mc * 128, lambda k: hn[:, k, :n], [hn_b], pb, n),
                lambda pb, mc=mc: headnorm(pb, n, cb + 26, QS[:, mc, :n], [QS_b])))
        chain(items)

    def mixer_out(l, g):
        cb = CVL * l
        c0, n = GROUPS[g]
        w3 = win_d[l].rearrange("(k p) n -> p k n", p=128)

        nq = max(1, n // 128)
        rq = min(n, 128)
        if g > 0:
            eref = 16 + 4 * (g - 1) + 1 + 2
            kb.op(dve, lambda e: e.tensor_tensor(
                out=bias_g[:], in0=WE[:], in1=E_all[:, eref:eref + 1, :].broadcast_to([128, 33, 8]),
                op=ALU.subtract), reads=[WE_b, E_b], writes=[biasg_b, totp_b])
        itile = [0]
        stages = []

        def emit_load(h):
            ch, p0 = h // 2, (h % 2) * 64
            s = (g * 8 + h) % 2
            kb.dma(sp, pKV_st[s], lambda e: e.dma_start(
                out=pK[s][:, :], in_=xout_d[l][0][0:128, ch * 2048:(ch + 1) * 2048]),
                reads=[xd_b[l][0]], writes=[pKV_b[s]])
            vc = 1 + h // 4
            kb.dma(sp, pKV_st[s], lambda e: e.dma_start(
                out=pV[s][:].rearrange("p b e -> p (b e)"),
                in_=xout_d[l][vc][0:128, (h % 4) * 1040:(h % 4 + 1) * 1040]),
                reads=[xd_b[l][vc]])
            pKV_b[s].lw = (pKV_st[s], pKV_st[s].count)

        def fox_head(h):
            ch, p0 = h // 2, (h % 2) * 64
            ob = 5 + h % 2
            tiles = []
            if g == 0:
                tiles.append((KT[:, ch, 0:16], [KT_b], VX[:16, h, 0, :], [VX_b],
                              wk_all[:16, 0, h:h + 1], [wk_b], 16, 0, 0))
            else:
                s = (g * 8 + h) % 2
                tiles.append((KT[:, ch, 0:128], [KT_b], VX[:, h, 0, :], [VX_b],
                              bias_g[:, 0, h:h + 1], [biasg_b], 128, 0, None))
                for jb in range(16):
                    tiles.append((pK[s][:, jb * 128:(jb + 1) * 128], [pKV_b[s]], pV[s][:, jb, :],
                                  [pKV_b[s]], bias_g[:, 1 + jb, h:h + 1], [biasg_b], 128, 0, None))
                for lb in range(1, 4 * g + 1):
                    d = lb - (4 * (g - 1) + 1)
                    kc0 = NMETA + (lb - 1) * 128
                    tiles.append((KT[:, ch, kc0:kc0 + 128], [KT_b], VX[:, h, lb, :], [VX_b],
                                  bias_g[:, 16 + lb, h:h + 1], [biasg_b], 128,
                                  max(d, 0) * 128, d if d >= 0 else None))
            slots = {}
            ntl = len(tiles)

            def mk(ti, tile):
                ksrc, kbufs, vsrc, vbufs, bsrc, bbufs, rows, q0, diag = tile

                def front():
                    if g > 0:
                        if h == 0 and ti == 0:
                            emit_load(0)
                        if ti == 2 and h + 1 < 8:
                            emit_load(h + 1)
                    it = itile[0]
                    itile[0] += 1
                    sb_, ps_ = it % 3, (3, 4, 7)[it % 3]
                    slots[ti] = sb_
                    kb.op(pe, lambda e: e.matmul(
                        out=pbank[ps_][:rows, q0:n], lhsT=ksrc, rhs=QFp[:, h, q0:n], start=True, stop=True),
                        reads=kbufs + [QF_b], writes=[pbank_b[ps_]])
                    kb.op(act, lambda e: e.activation(
                        out=PT[sb_][:rows, q0:n], in_=pbank[ps_][:rows, q0:n], func=AF.Exp, scale=SCALE, bias=bsrc),
                        reads=[pbank_b[ps_]] + bbufs, writes=[PT_b[sb_]])
                    if diag is not None:
                        kb.op(dve, lambda e: e.tensor_tensor(
                            out=PT[sb_][:rows, q0:q0 + rq], in0=PT[sb_][:rows, q0:q0 + rq], in1=tri_bf[:rows, :rq],
                            op=ALU.mult), reads=[PT_b[sb_], cbf_b], writes=[PT_b[sb_]])

                def back():
                    sb_ = slots[ti]
                    last_tile = (ti == ntl - 1)
                    for qb in range(q0 // 128, nq):
                        first = (ti == 0 and qb == q0 // 128)
                        kb.op(pe, lambda e, qb=qb, first=first: e.matmul(
                            out=pbank[ob][:rq, qb * 65:(qb + 1) * 65], lhsT=PT[sb_][:rows, qb * 128:qb * 128 + rq],
                            rhs=vsrc, start=first, stop=(last_tile and qb == nq - 1)),
                            reads=[PT_b[sb_]] + vbufs, writes=[pbank_b[ob]], sig=(qb == nq - 1))
                    if last_tile:
                        O3 = pbank[ob][:rq, :nq * 65].rearrange("p (q e) -> p q e", q=nq)
                        kb.op(dve, lambda e: e.tensor_scalar(out=den[:rq, :nq], in0=O3[:, :, 64], scalar1=1e-30,
                                                             scalar2=None, op0=ALU.max),
                              reads=[pbank_b[ob]], writes=[den_b])
                        kb.op(dve, lambda e: e.reciprocal(out=rcp[:rq, :nq], in_=den[:rq, :nq]),
                              reads=[den_b], writes=[rcp_b])
                        kb.op(dve, lambda e: e.tensor_tensor(
                            out=otf[:rq, :nq, h * 64:(h + 1) * 64], in0=O3[:, :, 0:64],
                            in1=rcp[:rq, :nq].unsqueeze(2).broadcast_to([rq, nq, 64]), op=ALU.mult),
                            reads=[pbank_b[ob], rcp_b], writes=[otf_b])
                return front, back

            for ti, tile in enumerate(tiles):
                stages.append(mk(ti, tile))

        for h in range(8):
            fox_head(h)

        def swa_blk(qb, g2):
            lb = 0 if g == 0 else 4 * (g - 1) + 1 + qb
            p0 = g2 * 64
            ob = 5 + g2
            tiles = []
            if g == 0:
                tiles.append((KS[p0:p0 + 64, 0:16], [KS_b], VSX[:16, g2, 0, :], [VSX_b],
                              metaq[:16, g2 * 4:(g2 + 1) * 4, :], 16))
            else:
                kc0 = NMETA + (lb - 1) * 128
                if lb == 1:
                    tiles.append((KSb[p0:p0 + 64, :], [bnd_b], VSb[:, g2, :], [bnd_b],
                                  swab[:, 0, g2 * 4:(g2 + 1) * 4, :], 128))
                else:
                    tiles.append((KS[p0:p0 + 64, kc0 - 128:kc0], [KS_b], VSX[:, g2, lb - 1, :], [VSX_b],
                                  swab[:, 0, g2 * 4:(g2 + 1) * 4, :], 128))
                tiles.append((KS[p0:p0 + 64, kc0:kc0 + 128], [KS_b], VSX[:, g2, lb, :], [VSX_b],
                              swab[:, 1, g2 * 4:(g2 + 1) * 4, :], 128))
                tiles.append((KS[p0:p0 + 64, 0:16], [KS_b], VSX[:16, g2, 0, :], [VSX_b],
                              metab[:16, g2 * 4:(g2 + 1) * 4, 0:128] if lb == 1 else
                              metab[:16, g2 * 4:(g2 + 1) * 4, 128:129].broadcast_to([16, 4, 128]), 16))
            slots = {}
            ntl = len(tiles)

            def mk(ti, tile):
                ksrc, kbufs, vsrc, vbufs, bsrc, rows = tile

                def front():
                    it = itile[0]
                    itile[0] += 1
                    sb_, ps_, ss_ = it % 3, (3, 4, 7)[it % 3], it % 2
                    slots[ti] = sb_
                    for r in range(4):
                        kb.op(pe, lambda e, r=r: e.matmul(
                            out=pbank[ps_][:rows, r * rq:(r + 1) * rq], lhsT=ksrc,
                            rhs=QS[p0:p0 + 64, r, qb * 128:qb * 128 + rq], start=True, stop=True),
                            reads=kbufs + [QS_b], writes=[pbank_b[ps_]], sig=(r == 3))
                    kb.op(dve, lambda e: e.scalar_tensor_tensor(
                        out=sT[ss_][:rows, :4 * rq].rearrange("p (r q) -> p r q", r=4),
                        in0=pbank[ps_][:rows, :4 * rq].rearrange("p (r q) -> p r q", r=4), scalar=SCALE,
                        in1=bsrc, op0=ALU.mult, op1=ALU.add),
                        reads=[pbank_b[ps_], bias_b], writes=[sT_b[ss_]])
                    if g > 0 and lb == 1 and ti == 0:
                        kb.op(dve, lambda e: e.tensor_scalar(
                            out=sT[ss_][:, :512], in0=sT[ss_][:, :512], scalar1=cvec[:, CV_MASK:CV_MASK + 1],
                            scalar2=None, op0=ALU.add), reads=[sT_b[ss_], cvec_b], writes=[sT_b[ss_]])
                    kb.op(act, lambda e: e.activation(
                        out=PT[sb_][:rows, :4 * rq], in_=sT[ss_][:rows, :4 * rq], func=AF.Exp),
                        reads=[sT_b[ss_]], writes=[PT_b[sb_]])

                def back():
                    sb_ = slots[ti]
                    last_tile = (ti == ntl - 1)
                    for r in range(4):
                        first = (ti == 0 and r == 0)
                        kb.op(pe, lambda e, r=r, first=first: e.matmul(
                            out=pbank[ob][:rq, r * 65:(r + 1) * 65], lhsT=PT[sb_][:rows, r * rq:(r + 1) * rq],
                            rhs=vsrc, start=first, stop=(last_tile and r == 3)),
                            reads=[PT_b[sb_]] + vbufs, writes=[pbank_b[ob]], sig=(r == 3))
                    if last_tile:
                        O3 = pbank[ob][:rq, :260].rearrange("p (q e) -> p q e", q=4)
                        kb.op(dve, lambda e: e.tensor_tensor(out=den[:rq, :4], in0=O3[:, :, 64],
                                                             in1=esink[:rq, g2 * 4:(g2 + 1) * 4], op=ALU.add),
                              reads=[pbank_b[ob], esink_b], writes=[den_b])
                        kb.op(dve, lambda e: e.reciprocal(out=rcp[:rq, :4], in_=den[:rq, :4]),
                              reads=[den_b], writes=[rcp_b])
                        kb.op(dve, lambda e: e.tensor_tensor(
                            out=ots[:rq, qb, g2 * 256:(g2 + 1) * 256].rearrange("p (r d) -> p r d", r=4),
                            in0=O3[:, :, 0:64], in1=rcp[:rq, :4].unsqueeze(2).broadcast_to([rq, 4, 64]),
                            op=ALU.mult), reads=[pbank_b[ob], rcp_b], writes=[ots_b])
                return front, back

            for ti, tile in enumerate(tiles):
                stages.append(mk(ti, tile))

        for qb in range(nq):
            for g2 in range(2):
                swa_blk(qb, g2)

        DEPTH_P = 2
        for t_ in range(len(stages) + DEPTH_P):
            if t_ < len(stages):
                stages[t_][0]()
            if t_ - DEPTH_P >= 0:
                stages[t_ - DEPTH_P][1]()

        for (src, src_b, dst, dst_b) in ((otf, otf_b, oTf, oTf_b), (ots, ots_b, oTs, oTs_b)):
            for qb in range(nq):
                pb = qb % 2
                for c in range(4):
                    kb.op(pe, lambda e, src=src, qb=qb, c=c, pb=pb: e.transpose(
                        out=pbank_bf[pb][:, c * 128:c * 128 + rq], in_=src[:rq, qb, c * 128:(c + 1) * 128],
                        identity=ident_bf[:rq, :rq]),
                        reads=[src_b, cbf_b], writes=[pbank_b[pb]], sig=(c == 3))
                kb.op(act, lambda e, dst=dst, qb=qb, pb=pb: e.activation(
                    out=dst[:, :, qb * 128:qb * 128 + rq],
                    in_=pbank_bf[pb][:, :512].rearrange("p (c q) -> p c q", c=4)[:, :, :rq], func=AF.Copy),
                    reads=[pbank_b[pb]], writes=[dst_b])

        wbf3 = wbf_d[l].rearrange("(k p) n -> p k n", p=128)
        wbs3 = wbs_d[l].rearrange("(k p) n -> p k n", p=128)
        for mt in range(4):
            ca = 2312 + mt * 256
            wA, wA_b = load_w([(w3[:, :, ca:ca + 256], 0, 256), (w3[:, :, ca + 1024:ca + 1280], 256, 256)], KC, 512)
            wB, wB_b = load_w([(wbf3[:, :, mt * 256:(mt + 1) * 256], 0, 256),
                               (wbs3[:, :, mt * 256:(mt + 1) * 256], 256, 256)], 4, 512)
            for mm in range(2):
                m = mt * 2 + mm
                b0, b1, b2, b3 = (3, 4, 5, 6) if m % 2 == 0 else (7, 2, 0, 1)
                proj_fm(wA, wA_b, KC, mm * 128, lambda k: hn[:, k, :n], [hn_b], b0, n)
                proj_fm(wA, wA_b, KC, 256 + mm * 128, lambda k: hn[:, k, :n], [hn_b], b1, n)
                proj_fm(wB, wB_b, 4, mm * 128, lambda k: oTf[:, k, :n], [oTf_b], b2, n)
                proj_fm(wB, wB_b, 4, 256 + mm * 128, lambda k: oTs[:, k, :n], [oTs_b], b3, n)
                kb.op(act, lambda e, b0=b0: e.activation(out=sT[0][:, :n], in_=pbank[b0][:, :n], func=AF.Sigmoid),
                      reads=[pbank_b[b0]], writes=[sT_b[0]])
                kb.op(act, lambda e, b1=b1: e.activation(out=sT[1][:, :n], in_=pbank[b1][:, :n], func=AF.Sigmoid),
                      reads=[pbank_b[b1]], writes=[sT_b[1]])
                kb.op(dve, lambda e, b2=b2: e.tensor_tensor(out=t1[:, :n], in0=pbank[b2][:, :n], in1=sT[0][:, :n],
                                                            op=ALU.mult),
                      reads=[pbank_b[b2], sT_b[0]], writes=[t1_b])
                kb.op(dve, lambda e, b3=b3: e.tensor_tensor(out=sT[1][:, :n], in0=pbank[b3][:, :n],
                                                            in1=sT[1][:, :n], op=ALU.mult),
                      reads=[pbank_b[b3], sT_b[1]], writes=[sT_b[1]])
                kb.op(dve, lambda e, m=m: e.tensor_tensor(out=yT[:, m, :n], in0=t1[:, :n], in1=sT[1][:, :n],
                                                          op=ALU.add),
                      reads=[t1_b, sT_b[1]], writes=YT_bufs)
        wo3 = wout_d[l].rearrange("(k p) n -> p k n", p=128)
        for mt in range(2):
            wO, wO_b = load_w([(wo3[:, :, mt * 512:(mt + 1) * 512], 0, 512)], KC, 512)
            for mm in range(4):
                m = mt * 4 + mm
                pb = mm % 2
                proj_fm(wO, wO_b, KC, mm * 128, lambda k: yT[:, k, :n], YT_bufs, pb, n)
                kb.op(dve, lambda e, m=m, pb=pb: e.tensor_tensor(
                    out=hT[:, m, c0:c0 + n], in0=pbank[pb][:, :n], in1=hT[:, m, c0:c0 + n], op=ALU.add),
                    reads=[pbank_b[pb], hT_b[m][g]], writes=[hT_b[m][g]])

    ost = [kb.stream(f"ost{i}") for i in range(2)]

    def store_blocks(blks):
        for blk in blks:
            slot = blk % 2
            col0 = NMETA + blk * 128
            g = 1 + blk // 4
            for q in range(2):
                pb = q
                for kk in range(4):
                    k = q * 4 + kk
                    kb.op(pe, lambda e, k=k, kk=kk, pb=pb, col0=col0: e.transpose(
                        out=pbank[pb][:, kk * 128:(kk + 1) * 128], in_=hT[:, k, col0:col0 + 128],
                        identity=ident),
                        reads=[hT_b[k][g], cmat_b], writes=[pbank_b[pb]], sig=(kk == 3))
                if q == 0:
                    fn = lambda e, q=q, pb=pb, slot=slot: e.activation(
                        out=stg[slot][:, q * 512:(q + 1) * 512], in_=pbank[pb][:, :], func=AF.Copy)
                else:
                    fn = lambda e, q=q, pb=pb, slot=slot: e.tensor_copy(
                        out=stg[slot][:, q * 512:(q + 1) * 512], in_=pbank[pb][:, :])
                kb.op(act if q == 0 else dve, fn, reads=[pbank_b[pb]], writes=[stg_b[slot]])
            kb.dma(sp, ost[slot], lambda e, blk=blk, slot=slot: e.dma_start(
                out=out_d[blk * 128:(blk + 1) * 128, :], in_=stg[slot][:, :]), reads=[stg_b[slot]])

    for l in range(DEPTH):
        if stage >= 1:
            g1c = CVL * l + 0
            h0_ = (lambda: load_x(range(8, 16))) if l == 0 else None
            ffn((1, 2, 0), g1c, f1_in[l], f1_out[l], pre_normed=(l > 0),
                hoist=lambda g1c=g1c: ffn_norms((3, 4), g1c), hoist0=h0_)
            ffn((3, 4), g1c, f1_in[l], f1_out[l], pre_normed=True)
        if stage == 1:
            break
        kb.barrier()
        mixer_kv(l)
        if stage == 1.5:
            break
        exchange(l)
        mixer_pre(l, 0)
        exchange_cc(l)
        mixer_out(l, 0)
        mixer_pre(l, 1)
        exchange_E(l)
        for g in range(1, 5):
            if g > 1:
                mixer_pre(l, g)
            mixer_out(l, g)
        kb.barrier()
        if stage in (2, 1.8):
            break
        g2c = CVL * l + 16
        ffn((1, 2, 0), g2c, f2_in[l], f2_out[l], hoist=lambda g2c=g2c: ffn_norms((3, 4), g2c))
        nxt = (lambda c_=CVL * (l + 1): ffn_norms((1, 2, 0), c_)) if (l + 1 < DEPTH and stage > 3) else None
        if l == DEPTH - 1 and stage > 3:
            store_blocks(range(0, 8))
        ffn((3, 4), g2c, f2_in[l], f2_out[l], pre_normed=True, hoist=nxt)
        if stage == 3:
            break

    store_blocks(range(8, 16) if stage > 3 else range(16))
    for s in ost:
        kb._wait(sp, s, s.count)

    kb.emit()
    stack.close()
    return nc


def _t5_bucket(d):
    d = np.maximum(d, 0)
    nf = np.maximum(d, 1).astype(np.float32)
    large = 16 + (np.log(nf / np.float32(16)) / np.float32(np.log(128 / 16)) * np.float32(16)).astype(np.int32)
    large = np.minimum(large, 31)
    return np.where(d < 16, d, large)


def make_consts(inp, half):
    cv = np.zeros((128, 128), np.float32)
    p64 = np.arange(128) % 64
    for l in range(DEPTH):
        cb = CVL * l
        for i, nm in enumerate(("ffn1_norm", "mix_norm", "ffn2_norm")):
            cv[:, cb + 8 * i: cb + 8 * i + 8] = inp[nm][l].reshape(KC, 128).T
        for i, nm in enumerate(("fox_q_norm", "fox_k_norm", "swa_q_norm", "swa_k_norm")):
            cv[:, cb + 24 + i] = inp[nm][l][p64]
        cv[:, cb + 28:cb + 36] = inp["forget_bias"][l][None, :]
        cv[:, cb + 36:cb + 44] = inp["swa_sinks"][l][None, :]
    cv[:, CV_FLAG] = float(half)
    cv[:, CV_MASK] = (float(half) - 1.0) * BIG
    tab = inp["rel_bias_table"]
    kk = np.arange(128)[:, None]
    qq = np.arange(128)[None, :]
    swab = np.zeros((128, 2, 8, 128), np.float32)
    for a, d in enumerate((128 + qq - kk, qq - kk)):
        valid = (d >= 0) & (d < 128)
        g = tab[_t5_bucket(d)]
        swab[:, a] = np.where(valid[:, None, :], g.transpose(0, 2, 1), np.float32(-BIG))
    mk = np.arange(16)[:, None]
    metab = np.zeros((16, 8, 129), np.float32)
    d_first = (128 + qq) - (112 + mk) if half == 0 else np.full((16, 128), 1000)
    metab[:, :, 0:128] = tab[_t5_bucket(d_first)].transpose(0, 2, 1)
    metab[:, :, 128] = tab[_t5_bucket(np.full((16,), 1000))]
    mq = np.arange(16)[None, :]
    dq = mq - mk
    metaq = np.where((dq >= 0)[:, None, :], tab[_t5_bucket(dq)].transpose(0, 2, 1), np.float32(-BIG))
    return cv, swab.reshape(128, -1), metab.reshape(16, -1), np.ascontiguousarray(metaq.reshape(16, -1), np.float32)


def kernel(**inp):
    stage = float(os.environ.get("KSTAGE", "99"))
    inp = {k: np.asarray(v) for k, v in inp.items()}
    nc = build_program(stage)
    cmat = np.zeros((128, 384), np.float32)
    cmat[:, 0:128] = np.eye(128)
    cmat[:, 128:256] = np.triu(np.ones((128, 128)))
    cmat[:, 256:384] = np.kron(np.eye(2), np.ones((64, 64)))
    w_in = inp["w_in"].copy()
    perm = np.concatenate([np.arange(h * 64, (h + 1) * 64) for h in (0, 4, 1, 5, 2, 6, 3, 7)])
    w_in[:, :, 1544:2056] = inp["w_in"][:, :, 1544 + perm]
    w_vsf = np.ascontiguousarray(np.concatenate([inp["w_in"][:, :, 2184:2312], inp["w_in"][:, :, 1536:1544]], axis=2))
    consts = [make_consts(inp, half) for half in range(2)]
    in_maps = []
    for c in range(8):
        b, half = c // 2, c % 2
        cv, swab, metab, metaq = consts[half]
        in_maps.append({
            "x": np.ascontiguousarray(inp["x"][b, half * NOWN:(half + 1) * NOWN]),
            "meta": np.ascontiguousarray(inp["meta_tokens"]),
            "cmat": cmat, "cvec": cv, "swab": swab, "metab": metab, "metaq": metaq,
            "ffn1_w_in": inp["ffn1_w_in"], "ffn1_w_out": inp["ffn1_w_out"],
            "ffn2_w_in": inp["ffn2_w_in"], "ffn2_w_out": inp["ffn2_w_out"],
            "w_in": w_in, "w_branch_fox": inp["w_branch_fox"], "w_branch_swa": inp["w_branch_swa"],
            "w_out": inp["w_out"], "w_vsf": w_vsf,
        })
    res = run_bass_kernel_spmd(nc, in_maps, core_ids=list(range(8)))
    out = np.zeros((4, 4096, D), np.float32)
    for c in range(8):
        b, half = c // 2, c % 2
        out[b, half * NOWN:(half + 1) * NOWN] = res.results[c]["out"]
    return out
```

```python
import os
import numpy as np
import concourse.bass as bass
import concourse.mybir as mybir
from concourse.bass_utils import run_bass_kernel_spmd

F32 = mybir.dt.float32
BF16 = mybir.dt.bfloat16
AF = mybir.ActivationFunctionType
ALU = mybir.AluOpType

D = 1024
KC = 8
DFF = 2816
JC = 22
NMETA = 16
NOWN = 2048
T = NMETA + NOWN
GROUPS = [(0, 16), (16, 512), (528, 512), (1040, 512), (1552, 512)]
DEPTH = 2
EPS = 1e-6
D_IN = 4360


class Prod:
    def __init__(self, name, sem):
        self.name, self.sem, self.count = name, sem, 0


class Eng(Prod):
    def __init__(self, name, sem):
        super().__init__(name, sem)
        self.ops = []
        self.waited = {}
        self.pending = False


class Buf:
    __slots__ = ("name", "lw", "rd")

    def __init__(self, name):
        self.name, self.lw, self.rd = name, None, {}


class KB:
    def __init__(self, nc, stack):
        self.nc, self.stack = nc, stack
        self.engs = {}
        for n in ("tensor", "scalar", "vector", "gpsimd", "sync"):
            self.engs[n] = Eng(n, stack.enter_context(nc.semaphore("sem_" + n)))
        self.pe, self.act, self.dve = self.engs["tensor"], self.engs["scalar"], self.engs["vector"]
        self.pool, self.sp = self.engs["gpsimd"], self.engs["sync"]
        self.streams = []
        self.nbuf = 0

    def stream(self, name):
        p = Prod(name, self.stack.enter_context(self.nc.semaphore("st_" + name)))
        self.streams.append(p)
        return p

    def buf(self, name=None):
        self.nbuf += 1
        return Buf(name or f"b{self.nbuf}")

    def _wait(self, eng, prod, idx):
        if idx <= 0:
            return
        if prod is eng and eng is self.pe:
            return
        if eng.waited.get(prod, 0) >= idx:
            return
        assert idx <= prod.count, f"wait on unsignalled {prod.name} {idx}>{prod.count} from {eng.name}"
        eng.ops.append(("wait", prod, idx))
        eng.waited[prod] = idx

    def _deps(self, eng, reads, writes):
        for b in reads:
            if b.lw is not None:
                self._wait(eng, *b.lw)
        for b in writes:
            if b.lw is not None:
                self._wait(eng, *b.lw)
            for p, i in b.rd.items():
                self._wait(eng, p, i)

    def op(self, eng, fn, reads=(), writes=(), sig=True):
        self._deps(eng, reads, writes)
        if eng is not self.pe:
            sig = True
        eng.ops.append(("op", fn, eng if sig else None))
        if sig:
            eng.count += 1
            idx = eng.count
            eng.pending = False
        else:
            idx = eng.count + 1
            eng.pending = True
        for b in reads:
            b.rd[eng] = idx
        for b in writes:
            b.lw = (eng, idx)
            b.rd = {}

    def dma(self, eng, st, fn, reads=(), writes=()):
        self._deps(eng, reads, writes)
        eng.ops.append(("dma", fn, st))
        st.count += 16
        idx = st.count
        for b in reads:
            b.rd[st] = idx
        for b in writes:
            b.lw = (st, idx)
            b.rd = {}

    def barrier(self):
        prods = list(self.engs.values()) + self.streams
        for e in self.engs.values():
            assert not e.pending, e.name
        for e in self.engs.values():
            for p in prods:
                if p is e:
                    continue
                self._wait(e, p, p.count)

    def emit(self):
        nc = self.nc
        with nc.Block() as block:
            for n, e in self.engs.items():
                def body(engine, e=e):
                    for item in e.ops:
                        if item[0] == "wait":
                            engine.wait_ge(item[1].sem, item[2])
                        elif item[0] == "op":
                            ins = item[1](engine)
                            if item[2] is not None:
                                ins.then_inc(item[2].sem, 1)
                        elif item[0] == "cc":
                            item[1](engine).then_inc(item[2].sem)
                        else:
                            item[1](engine).then_inc(item[2].sem, 16)
                getattr(block, n)(body)


CVL = 48
CV_FLAG = 96
CV_MASK = 97
BIG = 30000.0
XW = [8192, 4160, 4418]
X_KSB, X_VSB = 4160, 4288
X_WK, X_TOT = 0, 128
SCALE = 0.125
SKIP = os.environ.get('KSKIP', '')


def build_program(stage):
    from contextlib import ExitStack
    nc = bass.Bass("TRN2", target_bir_lowering=False)
    stack = ExitStack()
    kb = KB(nc, stack)
    pe, act, dve, pool, sp = kb.pe, kb.act, kb.dve, kb.pool, kb.sp

    def din(name, shape):
        return nc.dram_tensor(name, list(shape), F32, kind="ExternalInput").ap()

    x_d = din("x", (NOWN, D))
    meta_d = din("meta", (NMETA, D))
    cmat_d = din("cmat", (128, 384))
    cvec_d = din("cvec", (128, 128))
    swab_d = din("swab", (128, 2 * 8 * 128))
    metab_d = din("metab", (16, 8 * 129))
    metaq_d = din("metaq", (16, 8 * 16))
    f1_in = din("ffn1_w_in", (DEPTH, D, 2 * DFF))
    f1_out = din("ffn1_w_out", (DEPTH, DFF, D))
    f2_in = din("ffn2_w_in", (DEPTH, D, 2 * DFF))
    f2_out = din("ffn2_w_out", (DEPTH, DFF, D))
    win_d = din("w_in", (DEPTH, D, D_IN))
    wbf_d = din("w_branch_fox", (DEPTH, 512, D))
    wbs_d = din("w_branch_swa", (DEPTH, 512, D))
    wout_d = din("w_out", (DEPTH, D, D))
    wvsf_d = din("w_vsf", (DEPTH, D, 136))
    out_d = nc.dram_tensor("out", [NOWN, D], F32, kind="ExternalOutput").ap()
    xin_d = [[nc.dram_tensor(f"xin{l}_{c}", [128, XW[c]], BF16).ap() for c in range(3)]
             + [nc.dram_tensor(f"xin{l}_3", [128, 256], F32).ap()] for l in range(DEPTH)]
    xout_d = [[nc.dram_tensor(f"xout{l}_{c}", [256, XW[c]], BF16).ap() for c in range(3)]
              + [nc.dram_tensor(f"xout{l}_3", [256, 256], F32).ap()] for l in range(DEPTH)]

    def sb(name, shape, dt):
        return stack.enter_context(nc.sbuf_tensor("s_" + name, list(shape), dt))

    def ps(name, shape=(128, 512), dt=F32):
        return stack.enter_context(nc.psum_tensor(name, list(shape), dt))

    B = kb.buf
    hT = sb("hT", (128, KC, T), F32)
    hT_b = [[B(f"hT{k}_{g}") for g in range(5)] for k in range(KC)]
    cmat = sb("cmat", (128, 384), F32)
    cmat_b = B("cmat")
    ident = cmat[:, 0:128]
    U_f = cmat[:, 128:256]
    cbf = sb("cbf", (128, 384), BF16)
    cbf_b = B("cbf")
    ident_bf, tri_bf, bd_bf = cbf[:, 0:128], cbf[:, 128:256], cbf[:, 256:384]
    cvec = sb("cvec", (128, 128), F32)
    cvec_b = B("cvec")
    ones_bf = sb("ones_bf", (128, 128), BF16)
    ones_f = sb("ones_f", (128, 128), F32)
    ones_b = B("ones")
    swab = sb("swab", (128, 2, 8, 128), F32)
    metab = sb("metab", (16, 8, 129), F32)
    metaq = sb("metaq", (16, 8, 16), F32)
    esink = sb("esink", (128, 8), F32)
    bias_b = B("biasconst")
    esink_b = B("esink")
    hn = sb("hn", (128, KC, 512), BF16)
    hn_b = B("hn")
    hn_g, hn_gb = hn, hn_b
    lnv = sb("lnv", (128, 512), F32)
    lnv_b = B("lnv")
    rstd = sb("rstd", (128, 512), F32)
    rstd_b = B("rstd")
    NH = 6
    wt_all = sb("wt", (128, NH * 2048), BF16)
    wt_b = [B(f"wt{i}") for i in range(NH)]
    wt_st = [kb.stream(f"wt{i}") for i in range(NH)]
    cst = kb.stream("const")

    AR = 44520
    arena = sb("arena", (128, AR), BF16)
    aoff = [0]

    def carve(n, dt=BF16):
        n16 = n if dt == BF16 else 2 * n
        o = aoff[0]
        aoff[0] += n16 + (n16 % 2)
        assert aoff[0] <= AR, aoff[0]
        v = arena[:, o:o + n16]
        return v if dt == BF16 else v.bitcast(F32)

    aoff[0] = 0
    PW = 1040
    aT = carve(JC * PW).rearrange("p (j n) -> p j n", j=JC)
    aT_b = [[B(f"aT{j}_{gi}") for gi in range(3)] for j in range(JC)]
    hnF = carve(KC * PW).rearrange("p (k n) -> p k n", k=KC)
    hnF_b = [B(f"hnF{gi}") for gi in range(3)]
    wo = [carve(JC * 256) for _ in range(2)]
    wo_b = [B(f"wo{i}") for i in range(2)]
    wo_st = [kb.stream(f"wo{i}") for i in range(2)]
    sg = [carve(512, F32) for _ in range(2)]
    sg_b = [B(f"sg{i}") for i in range(2)]
    stg = [wo[i][:, 0:2 * D].bitcast(F32) for i in range(2)]
    stg_b = wo_b
    stg_st = [kb.stream(f"stg{i}") for i in range(2)]
    aoff[0] = 0
    KT = carve(4 * T).rearrange("p (c t) -> p c t", c=4)
    KT_b = B("KT")
    VX = carve(8 * 17 * 65).rearrange("p (h b e) -> p h b e", h=8, b=17)
    VX_b = B("VX")
    KS = carve(T)
    KS_b = B("KS")
    VSX = carve(2 * 17 * 65).rearrange("p (h b e) -> p h b e", h=2, b=17)
    VSX_b = B("VSX")
    KSb = carve(128)
    VSb = carve(130).rearrange("p (h e) -> p h e", h=2)
    bnd_b = B("bnd")
    pK = [carve(2048) for _ in range(2)]
    pV = [carve(1040).rearrange("p (b e) -> p b e", b=16) for _ in range(2)]
    pKV_b = [B(f"pKV{i}") for i in range(2)]
    pKV_st = [kb.stream(f"pKV{i}") for i in range(2)]
    QFp = carve(8 * 512).rearrange("p (h n) -> p h n", h=8)
    QF_b = B("QFp")
    QSO = carve(8 * 512)
    QS = QSO[:, 0:2048].rearrange("p (c n) -> p c n", c=4)
    otf = QSO[:, 2048:4096].rearrange("p (q f) -> p q f", q=4)
    yT = QSO.rearrange("p (c n) -> p c n", c=8)
    QS_b, otf_b = B("QS"), B("otf")
    YT_bufs = [QS_b, otf_b]
    PT = [carve(512) for _ in range(3)]
    PT_b = [B(f"PT{i}") for i in range(3)]
    sqh, sqh_b = PT[2], PT_b[2]
    sT = [carve(512, F32) for _ in range(2)]
    sT_b = [B(f"sT{i}") for i in range(2)]
    t1, t1_b = lnv, lnv_b
    ots = carve(4 * 512).rearrange("p (q f) -> p q f", q=4)
    ots_b = B("ots")
    oTf = pK[0].rearrange("p (c n) -> p c n", c=4)
    oTs = pK[1].rearrange("p (c n) -> p c n", c=4)
    oTf_b, oTs_b = pKV_b[0], pKV_b[1]
    lf_all = carve(17 * 8, F32).rearrange("p (b h) -> p b h", b=17)
    lf_b = B("lf")
    zt = carve(8, F32)
    et = carve(8, F32)
    zt_b, et_b = B("zt"), B("et")
    wk_all = carve(33 * 8, F32).rearrange("p (b h) -> p b h", b=33)
    tot_all = carve(33 * 8, F32).rearrange("p (b h) -> p b h", b=33)
    E_all = carve(33 * 8, F32).rearrange("p (b h) -> p b h", b=33)
    WE = carve(33 * 8, F32).rearrange("p (b h) -> p b h", b=33)
    tmp8 = carve(8, F32)
    wk_b, tot_b, E_b, WE_b, tmp8_b = B("wk"), B("tot"), B("E"), B("WE"), B("tmp8")
    wkp_b, totp_b = B("wkp"), B("totp")
    bias_g, biasg_b = tot_all, tot_b
    den = carve(8, F32)
    rcp = carve(8, F32)
    den_b, rcp_b = B("den"), B("rcp")
    xst = [[kb.stream(f"xst{l}_{c}") for c in range(4)] for l in range(DEPTH)]
    ccs = [[kb.stream(f"cc{l}_{c}") for c in range(4)] for l in range(DEPTH)]
    xld = kb.stream("xld")
    xd_b = [[B(f"xd{l}_{c}") for c in range(4)] for l in range(DEPTH)]

    pbank = [ps(f"pb{i}") for i in range(8)]
    pbank_b = [B(f"pb{i}") for i in range(8)]
    pbank_bf = [p.bitcast(BF16) for p in pbank]

    kb.dma(sp, cst, lambda e: e.dma_start(out=cmat[:], in_=cmat_d), writes=[cmat_b])
    kb.dma(sp, cst, lambda e: e.dma_start(out=cvec[:], in_=cvec_d), writes=[cvec_b])
    kb.dma(sp, cst, lambda e: e.dma_start(out=swab[:].rearrange("p a h q -> p (a h q)"), in_=swab_d),
           writes=[bias_b])
    kb.dma(sp, cst, lambda e: e.dma_start(out=metab[:].rearrange("p h q -> p (h q)"), in_=metab_d),
           writes=[bias_b])
    kb.dma(sp, cst, lambda e: e.dma_start(out=metaq[:].rearrange("p h q -> p (h q)"), in_=metaq_d),
           writes=[bias_b])
    for b_ in (cmat_b, cvec_b, bias_b):
        b_.lw = (cst, cst.count)
    kb.op(dve, lambda e: e.memset(ones_bf[:], 1.0), writes=[ones_b])
    kb.op(dve, lambda e: e.memset(ones_f[:], 1.0), writes=[ones_b])
    kb.op(dve, lambda e: e.tensor_copy(out=cbf[:], in_=cmat[:]), reads=[cmat_b], writes=[cbf_b])

    def load_block(src_ap, nrows, col0, slot):
        kb.dma(sp, stg_st[slot], lambda e: e.dma_start(out=stg[slot][:nrows, :], in_=src_ap),
               writes=[stg_b[slot]])
        g = [i for i, (c0, n) in enumerate(GROUPS) if c0 <= col0 < c0 + n][0]
        for q in range(2):
            pb = q
            for kk in range(4):
                k = q * 4 + kk
                kb.op(pe, lambda e, k=k, kk=kk, pb=pb: e.transpose(
                    out=pbank[pb][:, kk * nrows:(kk + 1) * nrows],
                    in_=stg[slot][:nrows, k * 128:(k + 1) * 128],
                    identity=ident[:nrows, :nrows]),
                    reads=[stg_b[slot], cmat_b], writes=[pbank_b[pb]], sig=(kk == 3))
            if q == 0:
                fn = lambda e, q=q, pb=pb: e.activation(
                    out=hT[:, q * 4:(q + 1) * 4, col0:col0 + nrows],
                    in_=pbank[pb][:, :4 * nrows].rearrange("p (a n) -> p a n", a=4), func=AF.Copy)
            else:
                fn = lambda e, q=q, pb=pb: e.tensor_copy(
                    out=hT[:, q * 4:(q + 1) * 4, col0:col0 + nrows],
                    in_=pbank[pb][:, :4 * nrows].rearrange("p (a n) -> p a n", a=4))
            kb.op(act if q == 0 else dve, fn, reads=[pbank_b[pb]],
                  writes=[hT_b[k][g] for k in range(q * 4, q * 4 + 4)])

    def load_x(blks):
        for blk in blks:
            load_block(x_d[blk * 128:(blk + 1) * 128, :], 128, NMETA + blk * 128, (blk + 1) % 2)

    load_block(meta_d, NMETA, 0, 0)
    load_x(range(8))

    wt_rr = [0]

    class WB:
        pass

    def load_w(parts, kdim, cw):
        nh = 1 if kdim * cw <= 2048 else 2
        i = wt_rr[0] % NH
        if nh == 2 and i % 2 == 1:
            i = (i + 1) % NH
            wt_rr[0] += 1
        wt_rr[0] += nh
        bufs = [wt_b[i + d_] for d_ in range(nh)]
        st_ = wt_st[i]
        view = wt_all[:, i * 2048:i * 2048 + kdim * cw].rearrange("p (k n) -> p k n", k=kdim)
        for n_, (src3, coff, w_) in enumerate(parts):
            kb.dma(pool, st_, lambda e, src3=src3, coff=coff, w_=w_: e.dma_start(
                out=view[:, :src3.shape[1], coff:coff + w_], in_=src3),
                writes=bufs if n_ == 0 else [])
        for b_ in bufs:
            b_.lw = (st_, st_.count)
        return view, bufs

    wo_rr = [0]

    def load_wo(src3, kdim, cw):
        i = wo_rr[0] % 2
        wo_rr[0] += 1
        view = wo[i][:, :kdim * cw].rearrange("p (k n) -> p k n", k=kdim)
        kb.dma(pool, wo_st[i], lambda e: e.dma_start(out=view, in_=src3), writes=[wo_b[i]])
        return view, wo_b[i]

    def rmsnorm(g, gcol, hn=None, hn_b=None):
        if hn is None:
            hn, hn_b = hn_g, hn_gb
        c0, n = GROUPS[g]
        kb.op(act, lambda e: e.activation(out=hn[:, :, :n], in_=hT[:, :, c0:c0 + n], func=AF.Square),
              reads=[hT_b[k][g] for k in range(KC)], writes=[hn_b])
        pb = 2
        for k in range(KC):
            kb.op(pe, lambda e, k=k: e.matmul(out=pbank[pb][:, :n], lhsT=ones_bf[:], rhs=hn[:, k, :n],
                                              start=(k == 0), stop=(k == KC - 1)),
                  reads=[hn_b, ones_b], writes=[pbank_b[pb]], sig=(k == KC - 1))
        kb.op(act, lambda e: e.activation(out=lnv[:, :n], in_=pbank[pb][:, :n], func=AF.Ln,
                                          scale=1.0 / D, bias=EPS),
              reads=[pbank_b[pb]], writes=[lnv_b])
        kb.op(act, lambda e: e.activation(out=rstd[:, :n], in_=lnv[:, :n], func=AF.Exp, scale=-0.5),
              reads=[lnv_b], writes=[rstd_b])
        for k in range(KC):
            kb.op(dve, lambda e, k=k: e.scalar_tensor_tensor(
                out=hn[:, k, :n], in0=hT[:, k, c0:c0 + n], scalar=cvec[:, gcol + k:gcol + k + 1],
                in1=rstd[:, :n], op0=ALU.mult, op1=ALU.mult),
                reads=[hT_b[k][g], cvec_b, rstd_b], writes=[hn_b])

    def headnorm(pb, n, gcol, out_ap, out_bufs, halves=None):
        kb.op(act, lambda e: e.activation(out=sqh[:, :n], in_=pbank[pb][:, :n], func=AF.Square),
              reads=[pbank_b[pb]], writes=[sqh_b])
        kb.op(pe, lambda e: e.matmul(out=pbank[7][:, :n], lhsT=bd_bf, rhs=sqh[:, :n], start=True, stop=True),
              reads=[sqh_b, cbf_b], writes=[pbank_b[7]])
        kb.op(act, lambda e: e.activation(out=lnv[:, :n], in_=pbank[7][:, :n], func=AF.Ln,
                                          scale=1.0 / 64, bias=EPS),
              reads=[pbank_b[7]], writes=[lnv_b])
        kb.op(act, lambda e: e.activation(out=rstd[:, :n], in_=lnv[:, :n], func=AF.Exp, scale=-0.5),
              reads=[lnv_b], writes=[rstd_b])
        if halves is None:
            kb.op(dve, lambda e: e.scalar_tensor_tensor(
                out=out_ap, in0=pbank[pb][:, :n], scalar=cvec[:, gcol:gcol + 1], in1=rstd[:, :n],
                op0=ALU.mult, op1=ALU.mult),
                reads=[pbank_b[pb], cvec_b, rstd_b], writes=out_bufs)
        else:
            for hi, oap in enumerate(halves):
                ps_ = slice(hi * 64, (hi + 1) * 64)
                kb.op(dve, lambda e, oap=oap, ps_=ps_: e.scalar_tensor_tensor(
                    out=oap, in0=pbank[pb][ps_, :n], scalar=cvec[ps_, gcol:gcol + 1], in1=rstd[ps_, :n],
                    op0=ALU.mult, op1=ALU.mult),
                    reads=[pbank_b[pb], cvec_b, rstd_b], writes=out_bufs)

    def chain(items, banks=(3, 4, 5, 6)):
        prev = None
        for i, (pf_, nf_) in enumerate(items):
            pb = banks[i % len(banks)]
            pf_(pb)
            if prev is not None:
                prev()
            prev = (lambda pb=pb, nf_=nf_: nf_(pb))
        if prev is not None:
            prev()

    def proj_fm(wv, wv_b, kdim, mcol, rhs_fn, rhs_bufs, pb, n):
        for k in range(kdim):
            kb.op(pe, lambda e, k=k: e.matmul(out=pbank[pb][:, :n], lhsT=wv[:, k, mcol:mcol + 128],
                                              rhs=rhs_fn(k), start=(k == 0), stop=(k == kdim - 1)),
                  reads=list(wv_b) + rhs_bufs, writes=[pbank_b[pb]], sig=(k == kdim - 1))

    def ffn_views(groups):
        offs = []
        o = 0
        for g in groups:
            offs.append(o)
            o += GROUPS[g][1]
        return offs, [hnF[:, :, offs[gi]:offs[gi] + GROUPS[g][1]] for gi, g in enumerate(groups)]

    def ffn_norms(groups, gcol):
        _, views = ffn_views(groups)
        for gi, g in enumerate(groups):
            rmsnorm(g, gcol, views[gi], hnF_b[gi])

    def ffn(groups, gcol, w_in_l, w_out_l, pre_normed=False, hoist=None, hoist0=None):
        offs, views = ffn_views(groups)
        w_in3 = w_in_l.rearrange("(k p) n -> p k n", p=128)
        tiles = [(i * 256, 256) for i in range(11)]
        itc = [0]

        def gate_up(wg, wg_b, wu, wu_b, h0, jj, gi):
            g = groups[gi]
            j = h0 // 128 + jj
            n = GROUPS[g][1]
            v = views[gi]
            it = itc[0]
            itc[0] += 1
            pg, pu = 3 + 2 * (it % 2), 4 + 2 * (it % 2)
            s_ = it % 2
            proj_fm(wg, wg_b, KC, jj * 128, lambda k: v[:, k, :], [hnF_b[gi]], pg, n)
            proj_fm(wu, wu_b, KC, jj * 128, lambda k: v[:, k, :], [hnF_b[gi]], pu, n)
            kb.op(act, lambda e: e.activation(out=sg[s_][:, :n], in_=pbank[pg][:, :n], func=AF.Silu),
                  reads=[pbank_b[pg]], writes=[sg_b[s_]])
            kb.op(dve, lambda e: e.tensor_tensor(
                out=aT[:, j, offs[gi]:offs[gi] + n], in0=pbank[pu][:, :n], in1=sg[s_][:, :n], op=ALU.mult),
                reads=[pbank_b[pu], sg_b[s_]], writes=[aT_b[j][gi]])

        for ti_, (h0, hw) in enumerate(tiles):
            ng = len(groups)
            if ti_ == 0 and not pre_normed:
                rmsnorm(groups[0], gcol, views[0], hnF_b[0])
                if ng > 1:
                    rmsnorm(groups[1], gcol, views[1], hnF_b[1])
            wg, wg_b = load_w([(w_in3[:, :, h0:h0 + hw], 0, hw)], KC, hw)
            wu, wu_b = load_w([(w_in3[:, :, DFF + h0:DFF + h0 + hw], 0, hw)], KC, hw)
            if ti_ == 0:
                for gi in range(ng):
                    for jj in range(hw // 128):
                        gate_up(wg, wg_b, wu, wu_b, h0, jj, gi)
                    if gi + 2 < ng and not pre_normed:
                        rmsnorm(groups[gi + 2], gcol, views[gi + 2], hnF_b[gi + 2])
            else:
                for jj in range(hw // 128):
                    for gi in range(len(groups)):
                        gate_up(wg, wg_b, wu, wu_b, h0, jj, gi)
            if ti_ == 0 and hoist0 is not None:
                hoist0()
        w_out3 = w_out_l.rearrange("(j p) n -> p j n", p=128)
        it = 0
        for mt in range(4):
            wv, wv_b = load_wo(w_out3[:, :, mt * 256:(mt + 1) * 256], JC, 256)
            if mt == 0 and hoist is not None:
                hoist()
            for mm in range(2):
                m = mt * 2 + mm
                for gi, g in enumerate(groups):
                    c0, n = GROUPS[g]
                    o_ = offs[gi]
                    pb = it % 2
                    it += 1
                    for j in range(JC):
                        kb.op(pe, lambda e, j=j, mm=mm, pb=pb, wv=wv, n=n, o_=o_: e.matmul(
                            out=pbank[pb][:, :n], lhsT=wv[:, j, mm * 128:(mm + 1) * 128], rhs=aT[:, j, o_:o_ + n],
                            start=(j == 0), stop=(j == JC - 1)),
                            reads=[wv_b, aT_b[j][gi]], writes=[pbank_b[pb]], sig=(j == JC - 1))
                    kb.op(dve, lambda e, m=m, pb=pb, c0=c0, n=n: e.scalar_tensor_tensor(
                        out=hT[:, m, c0:c0 + n], in0=pbank[pb][:, :n], scalar=0.5, in1=hT[:, m, c0:c0 + n],
                        op0=ALU.mult, op1=ALU.add),
                        reads=[pbank_b[pb], hT_b[m][g]], writes=[hT_b[m][g]])

    def mixer_kv(l):
        cb = CVL * l
        w3 = win_d[l].rearrange("(k p) n -> p k n", p=128)
        kb.op(dve, lambda e: e.memset(VX[:, :, :, 64:65], 1.0), writes=[VX_b])
        kb.op(dve, lambda e: e.memset(VX[:, :, 0, :], 0.0), writes=[VX_b])
        kb.op(dve, lambda e: e.memset(VX[:16, :, 0, 64:65], 1.0), writes=[VX_b])
        kb.op(dve, lambda e: e.memset(VSX[:, :, :, 64:65], 1.0), writes=[VSX_b])
        kb.op(dve, lambda e: e.memset(lf_all[:, 0, :], 0.0), writes=[lf_b])
        kb.op(act, lambda e: e.activation(out=esink[:], in_=cvec[:, cb + 36:cb + 44], func=AF.Exp),
              reads=[cvec_b], writes=[esink_b])
        hn_alt = QFp.rearrange("p h n -> p (h n)").rearrange("p (k n) -> p k n", k=KC)
        hbufs = [(hn_g, hn_gb), (hn_alt, QF_b)]

        def kv_group(g):
            c0, n = GROUPS[g]
            hn, hn_b = hbufs[g % 2]
            if g == 0:
                rmsnorm(g, cb + 8, hn, hn_b)
            wk_, wkb_ = load_w([(w3[:, :, 512:1024], 0, 512)], KC, 512)
            wks_, wksb_ = load_w([(w3[:, :, 2056:2184], 0, 128)], KC, 128)
            items = []
            for mc in range(4):
                items.append((
                    lambda pb, mc=mc: proj_fm(wk_, wkb_, KC, mc * 128, lambda k: hn[:, k, :n], [hn_b], pb, n),
                    lambda pb, mc=mc: headnorm(pb, n, cb + 25, KT[:, mc, c0:c0 + n], [KT_b])))
            items.append((
                lambda pb: proj_fm(wks_, wksb_, KC, 0, lambda k: hn[:, k, :n], [hn_b], pb, n),
                lambda pb: headnorm(pb, n, cb + 27, KS[:, c0:c0 + n], [KS_b])))
            chain(items)
            if g + 1 < 5:
                rmsnorm(g + 1, cb + 8, *hbufs[(g + 1) % 2])
            wv_, wvb_ = load_w([(w3[:, :, 1024:1536], 0, 512)], KC, 512)
            if "f" in SKIP:
                return
            wf_, wfb_ = load_w([(wvsf_d[l].rearrange("(k p) n -> p k n", p=128), 0, 136)], KC, 136)
            nblk = max(1, n // 128)
            for bi in range(nblk):
                rows = min(n, 128)
                lb = 0 if g == 0 else 4 * (g - 1) + 1 + bi
                cs = bi * 128
                pv, pf = bi % 2, 2
                for k in range(KC):
                    kb.op(pe, lambda e, k=k, pv=pv, cs=cs, rows=rows: e.matmul(
                        out=pbank[pv][:rows, :512], lhsT=hn[:, k, cs:cs + rows], rhs=wv_[:, k, :],
                        start=(k == 0), stop=(k == KC - 1)),
                        reads=[hn_b] + wvb_, writes=[pbank_b[pv]], sig=(k == KC - 1))
                if "f" in SKIP:
                    continue
                kb.op(act, lambda e, pv=pv, rows=rows, lb=lb: e.activation(
                    out=VX[:rows, :, lb, 0:64], in_=pbank[pv][:rows, :512].rearrange("p (h d) -> p h d", h=8),
                    func=AF.Copy), reads=[pbank_b[pv]], writes=[VX_b])
                for k in range(KC):
                    kb.op(pe, lambda e, k=k, pf=pf, cs=cs, rows=rows: e.matmul(
                        out=pbank[pf][:rows, :136], lhsT=hn[:, k, cs:cs + rows], rhs=wf_[:, k, :],
                        start=(k == 0), stop=(k == KC - 1)),
                        reads=[hn_b] + wfb_, writes=[pbank_b[pf]], sig=(k == KC - 1))
                kb.op(act, lambda e, pf=pf, rows=rows, lb=lb: e.activation(
                    out=VSX[:rows, :, lb, 0:64], in_=pbank[pf][:rows, :128].rearrange("p (h d) -> p h d", h=2),
                    func=AF.Copy), reads=[pbank_b[pf]], writes=[VSX_b])
                if "a" in SKIP:
                    continue
                kb.op(act, lambda e, pf=pf, rows=rows: e.activation(
                    out=zt[:rows, :], in_=pbank[pf][:rows, 128:136], func=AF.Copy),
                    reads=[pbank_b[pf]], writes=[zt_b])
                kb.op(dve, lambda e, rows=rows: e.tensor_tensor(
                    out=et[:rows, :], in0=zt[:rows, :], in1=cvec[:rows, cb + 28:cb + 36], op=ALU.add),
                    reads=[zt_b, cvec_b], writes=[et_b])
                kb.op(act, lambda e, rows=rows: e.activation(out=zt[:rows, :], in_=et[:rows, :], func=AF.Exp,
                                                             scale=-1.0),
                      reads=[et_b], writes=[zt_b])
                kb.op(dve, lambda e, rows=rows: e.tensor_scalar(out=et[:rows, :], in0=zt[:rows, :], scalar1=1.0,
                                                                scalar2=None, op0=ALU.add),
                      reads=[zt_b], writes=[et_b])
                kb.op(act, lambda e, rows=rows, lb=lb: e.activation(out=lf_all[:rows, lb, :], in_=et[:rows, :],
                                                                    func=AF.Ln),
                      reads=[et_b], writes=[lf_b])
        for g in range(5):
            kv_group(g)
        kb.op(dve, lambda e: e.memset(QFp[:, :, :], 0.0), writes=[QF_b])
        if "c" in SKIP:
            return
        lf2 = lf_all[:].rearrange("p b h -> p (b h)") if False else lf_all.rearrange("p b h -> p (b h)")
        kb.op(pe, lambda e: e.matmul(out=pbank[0][:, :136], lhsT=U_f, rhs=lf2, start=True, stop=True),
              reads=[lf_b, cmat_b], writes=[pbank_b[0]])
        kb.op(pe, lambda e: e.matmul(out=pbank[1][:, :136], lhsT=ones_f[:], rhs=lf2, start=True, stop=True),
              reads=[lf_b, ones_b], writes=[pbank_b[1]])
        wk2 = wk_all.rearrange("p b h -> p (b h)")
        tot2 = tot_all.rearrange("p b h -> p (b h)")
        kb.op(dve, lambda e: e.tensor_copy(out=wk2[:, 0:8], in_=pbank[0][:, 0:8]),
              reads=[pbank_b[0]], writes=[wk_b])
        kb.op(dve, lambda e: e.tensor_copy(out=wk2[:, 136:264], in_=pbank[0][:, 8:136]),
              reads=[pbank_b[0]], writes=[wk_b])
        kb.op(act, lambda e: e.activation(out=tot2[:, 0:8], in_=pbank[1][:, 0:8], func=AF.Copy),
              reads=[pbank_b[1]], writes=[tot_b])
        kb.op(act, lambda e: e.activation(out=tot2[:, 136:264], in_=pbank[1][:, 8:136], func=AF.Copy),
              reads=[pbank_b[1]], writes=[tot_b])

    def exchange(l):
        xi = xin_d[l]
        def st(c, fn, reads):
            kb.dma(sp, xst[l][c], fn, reads=reads, writes=[])
        st(0, lambda e: e.dma_start(out=xi[0][:, :].rearrange("p (c t) -> p c t", c=4), in_=KT[:, :, NMETA:T]),
           [KT_b])
        st(1, lambda e: e.dma_start(out=xi[1][:, :].rearrange("p (h x) -> p h x", h=4),
                                    in_=VX[:, 0:4, 1:17, :].rearrange("p h b e -> p h (b e)")), [VX_b])
        st(2, lambda e: e.dma_start(out=xi[2][:, 0:4160].rearrange("p (h x) -> p h x", h=4),
                                    in_=VX[:, 4:8, 1:17, :].rearrange("p h b e -> p h (b e)")), [VX_b])
        st(3, lambda e: e.dma_start(
            out=xi[3][:, X_WK:X_WK + 128], in_=wk_all[:, 17:33, :].rearrange("p b h -> p (b h)")), [wk_b])
        st(3, lambda e: e.dma_start(
            out=xi[3][:, X_TOT:X_TOT + 128], in_=tot_all[:, 17:33, :].rearrange("p b h -> p (b h)")), [tot_b])
        st(2, lambda e: e.dma_start(out=xi[2][:, X_KSB:X_KSB + 128], in_=KS[:, T - 128:T]), [KS_b])
        st(2, lambda e: e.dma_start(
            out=xi[2][:, X_VSB:X_VSB + 130].rearrange("p (h e) -> p h e", h=2), in_=VSX[:, :, 16, :]), [VSX_b])
        for c in (3, 0, 1, 2):
            xd_b[l][c].lw = (xst[l][c], xst[l][c].count)

    def exchange_cc(l, chunks):
        xi = xin_d[l]
        for c in chunks:
            kb._deps(pool, [xd_b[l][c]], [])
            pool.ops.append(("cc", lambda e, c=c: e.collective_compute(
                "AllGather", ALU.bypass, replica_groups=[[0, 1], [2, 3], [4, 5], [6, 7]],
                ins=[xi[c]], outs=[xout_d[l][c]]), ccs[l][c]))
            ccs[l][c].count += 1
            xd_b[l][c].lw = (ccs[l][c], 1)

    def exchange_post(l):
        xo = xout_d[l][2]
        xo3 = xout_d[l][3]
        kb.dma(sp, xld, lambda e: e.dma_start(
            out=wk_all[:, 1:17, :].rearrange("p b h -> p (b h)"), in_=xo3[0:128, X_WK:X_WK + 128]),
            reads=[xd_b[l][3]], writes=[wkp_b])
        kb.dma(sp, xld, lambda e: e.dma_start(
            out=tot_all[:, 1:17, :].rearrange("p b h -> p (b h)"), in_=xo3[0:128, X_TOT:X_TOT + 128]),
            reads=[xd_b[l][3]], writes=[totp_b])
        kb.dma(sp, xld, lambda e: e.dma_start(out=KSb, in_=xo[0:128, X_KSB:X_KSB + 128]),
               reads=[xd_b[l][2]], writes=[bnd_b])
        kb.dma(sp, xld, lambda e: e.dma_start(
            out=VSb, in_=xo[0:128, X_VSB:X_VSB + 130].rearrange("p (h e) -> p h e", h=2)),
            reads=[xd_b[l][2]], writes=[bnd_b])
        for b_ in (wkp_b, totp_b, bnd_b):
            b_.lw = (xld, xld.count)

    def exchange_E(l):
        TT = ALU
        kb.op(dve, lambda e: e.memset(E_all[:, 0, :], 0.0), writes=[E_b])
        kb.op(dve, lambda e: e.tensor_copy(out=E_all[:, 1, :], in_=tot_all[:, 0, :]), reads=[tot_b], writes=[E_b])
        for j in range(2, 17):
            kb.op(dve, lambda e, j=j: e.tensor_tensor(out=E_all[:, j, :], in0=E_all[:, j - 1, :],
                                                      in1=tot_all[:, j - 1, :], op=TT.add),
                  reads=[E_b, tot_b, totp_b], writes=[E_b])
        kb.op(dve, lambda e: e.tensor_tensor(out=tmp8[:, :], in0=E_all[:, 16, :], in1=tot_all[:, 16, :], op=TT.add),
              reads=[E_b, tot_b, totp_b], writes=[tmp8_b])
        kb.op(dve, lambda e: e.tensor_tensor(out=tmp8[:, :], in0=tmp8[:, :], in1=tot_all[:, 0, :], op=TT.subtract),
              reads=[tmp8_b, tot_b], writes=[tmp8_b])
        kb.op(dve, lambda e: e.scalar_tensor_tensor(
            out=E_all[:, 17, :], in0=tmp8[:, :], scalar=cvec[:, CV_FLAG:CV_FLAG + 1], in1=tot_all[:, 0, :],
            op0=TT.mult, op1=TT.add), reads=[tmp8_b, cvec_b, tot_b], writes=[E_b])
        for j in range(18, 33):
            kb.op(dve, lambda e, j=j: e.tensor_tensor(out=E_all[:, j, :], in0=E_all[:, j - 1, :],
                                                      in1=tot_all[:, j - 1, :], op=TT.add),
                  reads=[E_b, tot_b, totp_b], writes=[E_b])
        kb.op(dve, lambda e: e.tensor_scalar(
            out=E_all[:, 1:17, :], in0=E_all[:, 1:17, :], scalar1=cvec[:, CV_MASK:CV_MASK + 1], scalar2=None,
            op0=TT.add), reads=[E_b, cvec_b], writes=[E_b])
        kb.op(dve, lambda e: e.tensor_tensor(out=WE[:], in0=wk_all[:], in1=E_all[:], op=TT.add),
              reads=[wk_b, wkp_b, E_b], writes=[WE_b])

    def mixer_pre(l, g):
        cb = CVL * l
        c0, n = GROUPS[g]
        w3 = win_d[l].rearrange("(k p) n -> p k n", p=128)
        rmsnorm(g, cb + 8)
        wq_, wqb_ = load_w([(w3[:, :, 0:512], 0, 512)], KC, 512)
        wq2_, wq2b_ = load_w([(w3[:, :, 1544:2056], 0, 512)], KC, 512)
        items = []
        for mc in range(4):
            items.append((
                lambda pb, mc=mc: proj_fm(wq_, wqb_, KC, mc * 128, lambda k: hn[:, k, :n], [hn_b], pb, n),
                lambda pb, mc=mc: headnorm(pb, n, cb + 24, None, [QF_b],
                                           halves=[QFp[0:64, 2 * mc, :n], QFp[64:128, 2 * mc + 1, :n]])))
        for mc in range(4):
            items.append((
                lambda pb, mc=mc: proj_fm(wq2_, wq2b_, KC, mc * 128, lambda k: hn[:, k, :n], [hn_b], pb, n),
                lambda pb, mc=mc: headnorm(pb, n, cb + 26, QS[:, mc, :n], [QS_b])))
        chain(items)

    def mixer_out(l, g):
        cb = CVL * l
        c0, n = GROUPS[g]
        w3 = win_d[l].rearrange("(k p) n -> p k n", p=128)

        nq = max(1, n // 128)
        rq = min(n, 128)
        if g > 0:
            eref = 16 + 4 * (g - 1) + 1 + 2
            kb.op(dve, lambda e: e.tensor_tensor(
                out=bias_g[:], in0=WE[:], in1=E_all[:, eref:eref + 1, :].broadcast_to([128, 33, 8]),
                op=ALU.subtract), reads=[WE_b, E_b], writes=[biasg_b, totp_b])
        itile = [0]
        stages = []

        def emit_load(h):
            ch, p0 = h // 2, (h % 2) * 64
            s = (g * 8 + h) % 2
            kb.dma(sp, pKV_st[s], lambda e: e.dma_start(
                out=pK[s][:, :], in_=xout_d[l][0][0:128, ch * 2048:(ch + 1) * 2048]),
                reads=[xd_b[l][0]], writes=[pKV_b[s]])
            vc = 1 + h // 4
            kb.dma(sp, pKV_st[s], lambda e: e.dma_start(
                out=pV[s][:].rearrange("p b e -> p (b e)"),
                in_=xout_d[l][vc][0:128, (h % 4) * 1040:(h % 4 + 1) * 1040]),
                reads=[xd_b[l][vc]])
            pKV_b[s].lw = (pKV_st[s], pKV_st[s].count)

        def fox_head(h):
            ch, p0 = h // 2, (h % 2) * 64
            ob = 5 + h % 2
            tiles = []
            if g == 0:
                tiles.append((KT[:, ch, 0:16], [KT_b], VX[:16, h, 0, :], [VX_b],
                              wk_all[:16, 0, h:h + 1], [wk_b], 16, 0, 0))
            else:
                s = (g * 8 + h) % 2
                tiles.append((KT[:, ch, 0:128], [KT_b], VX[:, h, 0, :], [VX_b],
                              bias_g[:, 0, h:h + 1], [biasg_b], 128, 0, None))
                for jb in range(16):
                    tiles.append((pK[s][:, jb * 128:(jb + 1) * 128], [pKV_b[s]], pV[s][:, jb, :],
                                  [pKV_b[s]], bias_g[:, 1 + jb, h:h + 1], [biasg_b], 128, 0, None))
                for lb in range(1, 4 * g + 1):
                    d = lb - (4 * (g - 1) + 1)
                    kc0 = NMETA + (lb - 1) * 128
                    tiles.append((KT[:, ch, kc0:kc0 + 128], [KT_b], VX[:, h, lb, :], [VX_b],
                                  bias_g[:, 16 + lb, h:h + 1], [biasg_b], 128,
                                  max(d, 0) * 128, d if d >= 0 else None))
            slots = {}
            ntl = len(tiles)

            def mk(ti, tile):
                ksrc, kbufs, vsrc, vbufs, bsrc, bbufs, rows, q0, diag = tile

                def front():
                    if g > 0:
                        if h == 0 and ti == 0:
                            emit_load(0)
                        if ti == 2 and h + 1 < 8:
                            emit_load(h + 1)
                    it = itile[0]
                    itile[0] += 1
                    sb_, ps_ = it % 3, (3, 4, 7)[it % 3]
                    slots[ti] = sb_
                    kb.op(pe, lambda e: e.matmul(
                        out=pbank[ps_][:rows, q0:n], lhsT=ksrc, rhs=QFp[:, h, q0:n], start=True, stop=True),
                        reads=kbufs + [QF_b], writes=[pbank_b[ps_]])
                    kb.op(act, lambda e: e.activation(
                        out=PT[sb_][:rows, q0:n], in_=pbank[ps_][:rows, q0:n], func=AF.Exp, scale=SCALE, bias=bsrc),
                        reads=[pbank_b[ps_]] + bbufs, writes=[PT_b[sb_]])
                    if diag is not None:
                        kb.op(dve, lambda e: e.tensor_tensor(
                            out=PT[sb_][:rows, q0:q0 + rq], in0=PT[sb_][:rows, q0:q0 + rq], in1=tri_bf[:rows, :rq],
                            op=ALU.mult), reads=[PT_b[sb_], cbf_b], writes=[PT_b[sb_]])

                def back():
                    sb_ = slots[ti]
                    last_tile = (ti == ntl - 1)
                    for qb in range(q0 // 128, nq):
                        first = (ti == 0 and qb == q0 // 128)
                        kb.op(pe, lambda e, qb=qb, first=first: e.matmul(
                            out=pbank[ob][:rq, qb * 65:(qb + 1) * 65], lhsT=PT[sb_][:rows, qb * 128:qb * 128 + rq],
                            rhs=vsrc, start=first, stop=(last_tile and qb == nq - 1)),
                            reads=[PT_b[sb_]] + vbufs, writes=[pbank_b[ob]], sig=(qb == nq - 1))
                    if last_tile:
                        O3 = pbank[ob][:rq, :nq * 65].rearrange("p (q e) -> p q e", q=nq)
                        kb.op(dve, lambda e: e.tensor_scalar(out=den[:rq, :nq], in0=O3[:, :, 64], scalar1=1e-30,
                                                             scalar2=None, op0=ALU.max),
                              reads=[pbank_b[ob]], writes=[den_b])
                        kb.op(dve, lambda e: e.reciprocal(out=rcp[:rq, :nq], in_=den[:rq, :nq]),
                              reads=[den_b], writes=[rcp_b])
                        kb.op(dve, lambda e: e.tensor_tensor(
                            out=otf[:rq, :nq, h * 64:(h + 1) * 64], in0=O3[:, :, 0:64],
                            in1=rcp[:rq, :nq].unsqueeze(2).broadcast_to([rq, nq, 64]), op=ALU.mult),
                            reads=[pbank_b[ob], rcp_b], writes=[otf_b])
                return front, back

            for ti, tile in enumerate(tiles):
                stages.append(mk(ti, tile))

        for h in range(8):
            fox_head(h)

        def swa_blk(qb, g2):
            lb = 0 if g == 0 else 4 * (g - 1) + 1 + qb
            p0 = g2 * 64
            ob = 5 + g2
            tiles = []
            if g == 0:
                tiles.append((KS[p0:p0 + 64, 0:16], [KS_b], VSX[:16, g2, 0, :], [VSX_b],
                              metaq[:16, g2 * 4:(g2 + 1) * 4, :], 16))
            else:
                kc0 = NMETA + (lb - 1) * 128
                if lb == 1:
                    tiles.append((KSb[p0:p0 + 64, :], [bnd_b], VSb[:, g2, :], [bnd_b],
                                  swab[:, 0, g2 * 4:(g2 + 1) * 4, :], 128))
                else:
                    tiles.append((KS[p0:p0 + 64, kc0 - 128:kc0], [KS_b], VSX[:, g2, lb - 1, :], [VSX_b],
                                  swab[:, 0, g2 * 4:(g2 + 1) * 4, :], 128))
                tiles.append((KS[p0:p0 + 64, kc0:kc0 + 128], [KS_b], VSX[:, g2, lb, :], [VSX_b],
                              swab[:, 1, g2 * 4:(g2 + 1) * 4, :], 128))
                tiles.append((KS[p0:p0 + 64, 0:16], [KS_b], VSX[:16, g2, 0, :], [VSX_b],
                              metab[:16, g2 * 4:(g2 + 1) * 4, 0:128] if lb == 1 else
                              metab[:16, g2 * 4:(g2 + 1) * 4, 128:129].broadcast_to([16, 4, 128]), 16))
            slots = {}
            ntl = len(tiles)

            def mk(ti, tile):
                ksrc, kbufs, vsrc, vbufs, bsrc, rows = tile

                def front():
                    it = itile[0]
                    itile[0] += 1
                    sb_, ps_, ss_ = it % 3, (3, 4, 7)[it % 3], it % 2
                    slots[ti] = sb_
                    for r in range(4):
                        kb.op(pe, lambda e, r=r: e.matmul(
                            out=pbank[ps_][:rows, r * rq:(r + 1) * rq], lhsT=ksrc,
                            rhs=QS[p0:p0 + 64, r, qb * 128:qb * 128 + rq], start=True, stop=True),
                            reads=kbufs + [QS_b], writes=[pbank_b[ps_]], sig=(r == 3))
                    kb.op(dve, lambda e: e.scalar_tensor_tensor(
                        out=sT[ss_][:rows, :4 * rq].rearrange("p (r q) -> p r q", r=4),
                        in0=pbank[ps_][:rows, :4 * rq].rearrange("p (r q) -> p r q", r=4), scalar=SCALE,
                        in1=bsrc, op0=ALU.mult, op1=ALU.add),
                        reads=[pbank_b[ps_], bias_b], writes=[sT_b[ss_]])
                    if g > 0 and lb == 1 and ti == 0:
                        kb.op(dve, lambda e: e.tensor_scalar(
                            out=sT[ss_][:, :512], in0=sT[ss_][:, :512], scalar1=cvec[:, CV_MASK:CV_MASK + 1],
                            scalar2=None, op0=ALU.add), reads=[sT_b[ss_], cvec_b], writes=[sT_b[ss_]])
                    kb.op(act, lambda e: e.activation(
                        out=PT[sb_][:rows, :4 * rq], in_=sT[ss_][:rows, :4 * rq], func=AF.Exp),
                        reads=[sT_b[ss_]], writes=[PT_b[sb_]])

                def back():
                    sb_ = slots[ti]
                    last_tile = (ti == ntl - 1)
                    for r in range(4):
                        first = (ti == 0 and r == 0)
                        kb.op(pe, lambda e, r=r, first=first: e.matmul(
                            out=pbank[ob][:rq, r * 65:(r + 1) * 65], lhsT=PT[sb_][:rows, r * rq:(r + 1) * rq],
                            rhs=vsrc, start=first, stop=(last_tile and r == 3)),
                            reads=[PT_b[sb_]] + vbufs, writes=[pbank_b[ob]], sig=(r == 3))
                    if last_tile:
                        O3 = pbank[ob][:rq, :260].rearrange("p (q e) -> p q e", q=4)
                        kb.op(dve, lambda e: e.tensor_tensor(out=den[:rq, :4], in0=O3[:, :, 64],
                                                             in1=esink[:rq, g2 * 4:(g2 + 1) * 4], op=ALU.add),
                              reads=[pbank_b[ob], esink_b], writes=[den_b])
                        kb.op(dve, lambda e: e.reciprocal(out=rcp[:rq, :4], in_=den[:rq, :4]),
                              reads=[den_b], writes=[rcp_b])
                        kb.op(dve, lambda e: e.tensor_tensor(
                            out=ots[:rq, qb, g2 * 256:(g2 + 1) * 256].rearrange("p (r d) -> p r d", r=4),
                            in0=O3[:, :, 0:64], in1=rcp[:rq, :4].unsqueeze(2).broadcast_to([rq, 4, 64]),
                            op=ALU.mult), reads=[pbank_b[ob], rcp_b], writes=[ots_b])
                return front, back

            for ti, tile in enumerate(tiles):
                stages.append(mk(ti, tile))

        for qb in range(nq):
            for g2 in range(2):
                swa_blk(qb, g2)

        DEPTH_P = 2
        for t_ in range(len(stages) + DEPTH_P):
            if t_ < len(stages):
                stages[t_][0]()
            if t_ - DEPTH_P >= 0:
                stages[t_ - DEPTH_P][1]()

        for (src, src_b, dst, dst_b) in ((otf, otf_b, oTf, oTf_b), (ots, ots_b, oTs, oTs_b)):
            for qb in range(nq):
                pb = qb % 2
                for c in range(4):
                    kb.op(pe, lambda e, src=src, qb=qb, c=c, pb=pb: e.transpose(
                        out=pbank_bf[pb][:, c * 128:c * 128 + rq], in_=src[:rq, qb, c * 128:(c + 1) * 128],
                        identity=ident_bf[:rq, :rq]),
                        reads=[src_b, cbf_b], writes=[pbank_b[pb]], sig=(c == 3))
                kb.op(act, lambda e, dst=dst, qb=qb, pb=pb: e.activation(
                    out=dst[:, :, qb * 128:qb * 128 + rq],
                    in_=pbank_bf[pb][:, :512].rearrange("p (c q) -> p c q", c=4)[:, :, :rq], func=AF.Copy),
                    reads=[pbank_b[pb]], writes=[dst_b])

        wbf3 = wbf_d[l].rearrange("(k p) n -> p k n", p=128)
        wbs3 = wbs_d[l].rearrange("(k p) n -> p k n", p=128)
        for mt in range(4):
            ca = 2312 + mt * 256
            wA, wA_b = load_w([(w3[:, :, ca:ca + 256], 0, 256), (w3[:, :, ca + 1024:ca + 1280], 256, 256)], KC, 512)
            wB, wB_b = load_w([(wbf3[:, :, mt * 256:(mt + 1) * 256], 0, 256),
                               (wbs3[:, :, mt * 256:(mt + 1) * 256], 256, 256)], 4, 512)
            for mm in range(2):
                m = mt * 2 + mm
                b0, b1, b2, b3 = (3, 4, 5, 6) if m % 2 == 0 else (7, 2, 0, 1)
                proj_fm(wA, wA_b, KC, mm * 128, lambda k: hn[:, k, :n], [hn_b], b0, n)
                proj_fm(wA, wA_b, KC, 256 + mm * 128, lambda k: hn[:, k, :n], [hn_b], b1, n)
                proj_fm(wB, wB_b, 4, mm * 128, lambda k: oTf[:, k, :n], [oTf_b], b2, n)
                proj_fm(wB, wB_b, 4, 256 + mm * 128, lambda k: oTs[:, k, :n], [oTs_b], b3, n)
                kb.op(act, lambda e, b0=b0: e.activation(out=sT[0][:, :n], in_=pbank[b0][:, :n], func=AF.Sigmoid),
                      reads=[pbank_b[b0]], writes=[sT_b[0]])
                kb.op(act, lambda e, b1=b1: e.activation(out=sT[1][:, :n], in_=pbank[b1][:, :n], func=AF.Sigmoid),
                      reads=[pbank_b[b1]], writes=[sT_b[1]])
                kb.op(dve, lambda e, b2=b2: e.tensor_tensor(out=t1[:, :n], in0=pbank[b2][:, :n], in1=sT[0][:, :n],
                                                            op=ALU.mult),
                      reads=[pbank_b[b2], sT_b[0]], writes=[t1_b])
                kb.op(dve, lambda e, b3=b3: e.tensor_tensor(out=sT[1][:, :n], in0=pbank[b3][:, :n],
                                                            in1=sT[1][:, :n], op=ALU.mult),
                      reads=[pbank_b[b3], sT_b[1]], writes=[sT_b[1]])
                kb.op(dve, lambda e, m=m: e.tensor_tensor(out=yT[:, m, :n], in0=t1[:, :n], in1=sT[1][:, :n],
                                                          op=ALU.add),
                      reads=[t1_b, sT_b[1]], writes=YT_bufs)
        wo3 = wout_d[l].rearrange("(k p) n -> p k n", p=128)
        for mt in range(2):
            wO, wO_b = load_w([(wo3[:, :, mt * 512:(mt + 1) * 512], 0, 512)], KC, 512)
            for mm in range(4):
                m = mt * 4 + mm
                pb = mm % 2
                proj_fm(wO, wO_b, KC, mm * 128, lambda k: yT[:, k, :n], YT_bufs, pb, n)
                kb.op(dve, lambda e, m=m, pb=pb: e.tensor_tensor(
                    out=hT[:, m, c0:c0 + n], in0=pbank[pb][:, :n], in1=hT[:, m, c0:c0 + n], op=ALU.add),
                    reads=[pbank_b[pb], hT_b[m][g]], writes=[hT_b[m][g]])

    ost = [kb.stream(f"ost{i}") for i in range(2)]

    def store_blocks(blks):
        for blk in blks:
            slot = blk % 2
            col0 = NMETA + blk * 128
            g = 1 + blk // 4
            for q in range(2):
                pb = q
                for kk in range(4):
                    k = q * 4 + kk
                    kb.op(pe, lambda e, k=k, kk=kk, pb=pb, col0=col0: e.transpose(
                        out=pbank[pb][:, kk * 128:(kk + 1) * 128], in_=hT[:, k, col0:col0 + 128],
                        identity=ident),
                        reads=[hT_b[k][g], cmat_b], writes=[pbank_b[pb]], sig=(kk == 3))
                if q == 0:
                    fn = lambda e, q=q, pb=pb, slot=slot: e.activation(
                        out=stg[slot][:, q * 512:(q + 1) * 512], in_=pbank[pb][:, :], func=AF.Copy)
                else:
                    fn = lambda e, q=q, pb=pb, slot=slot: e.tensor_copy(
                        out=stg[slot][:, q * 512:(q + 1) * 512], in_=pbank[pb][:, :])
                kb.op(act if q == 0 else dve, fn, reads=[pbank_b[pb]], writes=[stg_b[slot]])
            kb.dma(sp, ost[slot], lambda e, blk=blk, slot=slot: e.dma_start(
                out=out_d[blk * 128:(blk + 1) * 128, :], in_=stg[slot][:, :]), reads=[stg_b[slot]])

    for l in range(DEPTH):
        if stage >= 1:
            g1c = CVL * l + 0
            h0_ = (lambda: load_x(range(8, 16))) if l == 0 else None
            ffn((1, 2, 0), g1c, f1_in[l], f1_out[l], pre_normed=(l > 0),
                hoist=lambda g1c=g1c: ffn_norms((3, 4), g1c), hoist0=h0_)
            ffn((3, 4), g1c, f1_in[l], f1_out[l], pre_normed=True)
        if stage == 1:
            break
        kb.barrier()
        mixer_kv(l)
        if stage == 1.5:
            break
        exchange(l)
        mixer_pre(l, 0)
        exchange_cc(l, (3, 0))
        mixer_out(l, 0)
        exchange_cc(l, (1,))
        mixer_pre(l, 1)
        exchange_cc(l, (2,))
        exchange_post(l)
        exchange_E(l)
        for g in range(1, 5):
            if g > 1:
                mixer_pre(l, g)
            mixer_out(l, g)
        kb.barrier()
        if stage in (2, 1.8):
            break
        g2c = CVL * l + 16
        ffn((1, 2, 0), g2c, f2_in[l], f2_out[l], hoist=lambda g2c=g2c: ffn_norms((3, 4), g2c))
        nxt = (lambda c_=CVL * (l + 1): ffn_norms((1, 2, 0), c_)) if (l + 1 < DEPTH and stage > 3) else None
        if l == DEPTH - 1 and stage > 3:
            store_blocks(range(0, 8))
        ffn((3, 4), g2c, f2_in[l], f2_out[l], pre_normed=True, hoist=nxt)
        if stage == 3:
            break

    store_blocks(range(8, 16) if stage > 3 else range(16))
    for s in ost:
        kb._wait(sp, s, s.count)

    kb.emit()
    stack.close()
    return nc


def _t5_bucket(d):
    d = np.maximum(d, 0)
    nf = np.maximum(d, 1).astype(np.float32)
    large = 16 + (np.log(nf / np.float32(16)) / np.float32(np.log(128 / 16)) * np.float32(16)).astype(np.int32)
    large = np.minimum(large, 31)
    return np.where(d < 16, d, large)


def make_consts(inp, half):
    cv = np.zeros((128, 128), np.float32)
    p64 = np.arange(128) % 64
    for l in range(DEPTH):
        cb = CVL * l
        for i, nm in enumerate(("ffn1_norm", "mix_norm", "ffn2_norm")):
            cv[:, cb + 8 * i: cb + 8 * i + 8] = inp[nm][l].reshape(KC, 128).T
        for i, nm in enumerate(("fox_q_norm", "fox_k_norm", "swa_q_norm", "swa_k_norm")):
            cv[:, cb + 24 + i] = inp[nm][l][p64]
        cv[:, cb + 28:cb + 36] = inp["forget_bias"][l][None, :]
        cv[:, cb + 36:cb + 44] = inp["swa_sinks"][l][None, :]
    cv[:, CV_FLAG] = float(half)
    cv[:, CV_MASK] = (float(half) - 1.0) * BIG
    tab = inp["rel_bias_table"]
    kk = np.arange(128)[:, None]
    qq = np.arange(128)[None, :]
    swab = np.zeros((128, 2, 8, 128), np.float32)
    for a, d in enumerate((128 + qq - kk, qq - kk)):
        valid = (d >= 0) & (d < 128)
        g = tab[_t5_bucket(d)]
        swab[:, a] = np.where(valid[:, None, :], g.transpose(0, 2, 1), np.float32(-BIG))
    mk = np.arange(16)[:, None]
    metab = np.zeros((16, 8, 129), np.float32)
    d_first = (128 + qq) - (112 + mk) if half == 0 else np.full((16, 128), 1000)
    metab[:, :, 0:128] = tab[_t5_bucket(d_first)].transpose(0, 2, 1)
    metab[:, :, 128] = tab[_t5_bucket(np.full((16,), 1000))]
    mq = np.arange(16)[None, :]
    dq = mq - mk
    metaq = np.where((dq >= 0)[:, None, :], tab[_t5_bucket(dq)].transpose(0, 2, 1), np.float32(-BIG))
    return cv, swab.reshape(128, -1), metab.reshape(16, -1), np.ascontiguousarray(metaq.reshape(16, -1), np.float32)


def kernel(**inp):
    stage = float(os.environ.get("KSTAGE", "99"))
    inp = {k: np.asarray(v) for k, v in inp.items()}
    nc = build_program(stage)
    cmat = np.zeros((128, 384), np.float32)
    cmat[:, 0:128] = np.eye(128)
    cmat[:, 128:256] = np.triu(np.ones((128, 128)))
    cmat[:, 256:384] = np.kron(np.eye(2), np.ones((64, 64)))
    w_in = inp["w_in"].copy()
    perm = np.concatenate([np.arange(h * 64, (h + 1) * 64) for h in (0, 4, 1, 5, 2, 6, 3, 7)])
    w_in[:, :, 1544:2056] = inp["w_in"][:, :, 1544 + perm]
    w_vsf = np.ascontiguousarray(np.concatenate([inp["w_in"][:, :, 2184:2312], inp["w_in"][:, :, 1536:1544]], axis=2))
    consts = [make_consts(inp, half) for half in range(2)]
    in_maps = []
    for c in range(8):
        b, half = c // 2, c % 2
        cv, swab, metab, metaq = consts[half]
        in_maps.append({
            "x": np.ascontiguousarray(inp["x"][b, half * NOWN:(half + 1) * NOWN]),
            "meta": np.ascontiguousarray(inp["meta_tokens"]),
            "cmat": cmat, "cvec": cv, "swab": swab, "metab": metab, "metaq": metaq,
            "ffn1_w_in": inp["ffn1_w_in"], "ffn1_w_out": inp["ffn1_w_out"],
            "ffn2_w_in": inp["ffn2_w_in"], "ffn2_w_out": inp["ffn2_w_out"],
            "w_in": w_in, "w_branch_fox": inp["w_branch_fox"], "w_branch_swa": inp["w_branch_swa"],
            "w_out": inp["w_out"], "w_vsf": w_vsf,
        })
    res = run_bass_kernel_spmd(nc, in_maps, core_ids=list(range(8)))
    out = np.zeros((4, 4096, D), np.float32)
    for c in range(8):
        b, half = c // 2, c % 2
        out[b, half * NOWN:(half + 1) * NOWN] = res.results[c]["out"]
    return out
```

```python
import os
import numpy as np
import concourse.bass as bass
import concourse.mybir as mybir
from concourse.bass_utils import run_bass_kernel_spmd

F32 = mybir.dt.float32
BF16 = mybir.dt.bfloat16
AF = mybir.ActivationFunctionType
ALU = mybir.AluOpType

D = 1024
KC = 8
DFF = 2816
JC = 22
NMETA = 16
NOWN = 2048
T = NMETA + NOWN
GROUPS = [(0, 16), (16, 512), (528, 512), (1040, 512), (1552, 512)]
DEPTH = 2
EPS = 1e-6
D_IN = 4360


class Prod:
    def __init__(self, name, sem):
        self.name, self.sem, self.count = name, sem, 0


class Eng(Prod):
    def __init__(self, name, sem):
        super().__init__(name, sem)
        self.ops = []
        self.waited = {}
        self.pending = False


class Buf:
    __slots__ = ("name", "lw", "rd")

    def __init__(self, name):
        self.name, self.lw, self.rd = name, None, {}


class KB:
    def __init__(self, nc, stack):
        self.nc, self.stack = nc, stack
        self.engs = {}
        for n in ("tensor", "scalar", "vector", "gpsimd", "sync"):
            self.engs[n] = Eng(n, stack.enter_context(nc.semaphore("sem_" + n)))
        self.pe, self.act, self.dve = self.engs["tensor"], self.engs["scalar"], self.engs["vector"]
        self.pool, self.sp = self.engs["gpsimd"], self.engs["sync"]
        self.streams = []
        self.nbuf = 0

    def stream(self, name):
        p = Prod(name, self.stack.enter_context(self.nc.semaphore("st_" + name)))
        self.streams.append(p)
        return p

    def buf(self, name=None):
        self.nbuf += 1
        return Buf(name or f"b{self.nbuf}")

    def _wait(self, eng, prod, idx):
        if idx <= 0:
            return
        if prod is eng and eng is self.pe:
            return
        if eng.waited.get(prod, 0) >= idx:
            return
        assert idx <= prod.count, f"wait on unsignalled {prod.name} {idx}>{prod.count} from {eng.name}"
        eng.ops.append(("wait", prod, idx))
        eng.waited[prod] = idx

    def _deps(self, eng, reads, writes):
        for b in reads:
            if b.lw is not None:
                self._wait(eng, *b.lw)
        for b in writes:
            if b.lw is not None:
                self._wait(eng, *b.lw)
            for p, i in b.rd.items():
                self._wait(eng, p, i)

    def op(self, eng, fn, reads=(), writes=(), sig=True):
        self._deps(eng, reads, writes)
        if eng is not self.pe:
            sig = True
        eng.ops.append(("op", fn, eng if sig else None))
        if sig:
            eng.count += 1
            idx = eng.count
            eng.pending = False
        else:
            idx = eng.count + 1
            eng.pending = True
        for b in reads:
            b.rd[eng] = idx
        for b in writes:
            b.lw = (eng, idx)
            b.rd = {}

    def dma(self, eng, st, fn, reads=(), writes=()):
        self._deps(eng, reads, writes)
        eng.ops.append(("dma", fn, st))
        st.count += 16
        idx = st.count
        for b in reads:
            b.rd[st] = idx
        for b in writes:
            b.lw = (st, idx)
            b.rd = {}

    def barrier(self):
        prods = list(self.engs.values()) + self.streams
        for e in self.engs.values():
            assert not e.pending, e.name
        for e in self.engs.values():
            for p in prods:
                if p is e:
                    continue
                self._wait(e, p, p.count)

    def emit(self):
        nc = self.nc
        with nc.Block() as block:
            for n, e in self.engs.items():
                def body(engine, e=e):
                    for item in e.ops:
                        if item[0] == "wait":
                            engine.wait_ge(item[1].sem, item[2])
                        elif item[0] == "op":
                            ins = item[1](engine)
                            if item[2] is not None:
                                ins.then_inc(item[2].sem, 1)
                        elif item[0] == "cc":
                            item[1](engine).then_inc(item[2].sem)
                        else:
                            item[1](engine).then_inc(item[2].sem, 16)
                getattr(block, n)(body)


CVL = 48
CV_FLAG = 96
CV_MASK = 97
BIG = 30000.0
XW = [8192, 4160, 4418]
X_KSB, X_VSB = 4160, 4288
X_WK, X_TOT = 0, 128
SCALE = 0.125
SKIP = os.environ.get('KSKIP', '')


def build_program(stage):
    from contextlib import ExitStack
    nc = bass.Bass("TRN2", target_bir_lowering=False)
    stack = ExitStack()
    kb = KB(nc, stack)
    pe, act, dve, pool, sp = kb.pe, kb.act, kb.dve, kb.pool, kb.sp

    def din(name, shape):
        return nc.dram_tensor(name, list(shape), F32, kind="ExternalInput").ap()

    x_d = din("x", (NOWN, D))
    meta_d = din("meta", (NMETA, D))
    cmat_d = din("cmat", (128, 384))
    cvec_d = din("cvec", (128, 128))
    swab_d = din("swab", (128, 2 * 8 * 128))
    metab_d = din("metab", (16, 8 * 129))
    metaq_d = din("metaq", (16, 8 * 16))
    f1_in = din("ffn1_w_in", (DEPTH, D, 2 * DFF))
    f1_out = din("ffn1_w_out", (DEPTH, DFF, D))
    f2_in = din("ffn2_w_in", (DEPTH, D, 2 * DFF))
    f2_out = din("ffn2_w_out", (DEPTH, DFF, D))
    win_d = din("w_in", (DEPTH, D, D_IN))
    wbf_d = din("w_branch_fox", (DEPTH, 512, D))
    wbs_d = din("w_branch_swa", (DEPTH, 512, D))
    wout_d = din("w_out", (DEPTH, D, D))
    wvsf_d = din("w_vsf", (DEPTH, D, 136))
    out_d = nc.dram_tensor("out", [NOWN, D], F32, kind="ExternalOutput").ap()
    xin_d = [[nc.dram_tensor(f"xin{l}_{c}", [128, XW[c]], BF16).ap() for c in range(3)]
             + [nc.dram_tensor(f"xin{l}_3", [128, 256], F32).ap()] for l in range(DEPTH)]
    xout_d = [[nc.dram_tensor(f"xout{l}_{c}", [256, XW[c]], BF16).ap() for c in range(3)]
              + [nc.dram_tensor(f"xout{l}_3", [256, 256], F32).ap()] for l in range(DEPTH)]

    def sb(name, shape, dt):
        return stack.enter_context(nc.sbuf_tensor("s_" + name, list(shape), dt))

    def ps(name, shape=(128, 512), dt=F32):
        return stack.enter_context(nc.psum_tensor(name, list(shape), dt))

    B = kb.buf
    hT = sb("hT", (128, KC, T), F32)
    hT_b = [[B(f"hT{k}_{g}") for g in range(5)] for k in range(KC)]
    cmat = sb("cmat", (128, 384), F32)
    cmat_b = B("cmat")
    ident = cmat[:, 0:128]
    U_f = cmat[:, 128:256]
    cbf = sb("cbf", (128, 384), BF16)
    cbf_b = B("cbf")
    ident_bf, tri_bf, bd_bf = cbf[:, 0:128], cbf[:, 128:256], cbf[:, 256:384]
    cvec = sb("cvec", (128, 128), F32)
    cvec_b = B("cvec")
    ones_bf = sb("ones_bf", (128, 128), BF16)
    ones_f = sb("ones_f", (128, 128), F32)
    ones_b = B("ones")
    swab = sb("swab", (128, 2, 8, 128), F32)
    metab = sb("metab", (16, 8, 129), F32)
    metaq = sb("metaq", (16, 8, 16), F32)
    esink = sb("esink", (128, 8), F32)
    bias_b = B("biasconst")
    esink_b = B("esink")
    hn = sb("hn", (128, KC, 512), BF16)
    hn_b = B("hn")
    hn_g, hn_gb = hn, hn_b
    lnv = sb("lnv", (128, 512), F32)
    lnv_b = B("lnv")
    rstd = sb("rstd", (128, 512), F32)
    rstd_b = B("rstd")
    NH = 6
    wt_all = sb("wt", (128, NH * 2048), BF16)
    wt_b = [B(f"wt{i}") for i in range(NH)]
    wt_st = [kb.stream(f"wt{i}") for i in range(NH)]
    cst = kb.stream("const")

    AR = 44520
    arena = sb("arena", (128, AR), BF16)
    aoff = [0]

    def carve(n, dt=BF16):
        n16 = n if dt == BF16 else 2 * n
        o = aoff[0]
        aoff[0] += n16 + (n16 % 2)
        assert aoff[0] <= AR, aoff[0]
        v = arena[:, o:o + n16]
        return v if dt == BF16 else v.bitcast(F32)

    aoff[0] = 0
    PW = 1040
    aT = carve(JC * PW).rearrange("p (j n) -> p j n", j=JC)
    aT_b = [[B(f"aT{j}_{gi}") for gi in range(3)] for j in range(JC)]
    hnF = carve(KC * PW).rearrange("p (k n) -> p k n", k=KC)
    hnF_b = [B(f"hnF{gi}") for gi in range(3)]
    wo = [carve(JC * 256) for _ in range(2)]
    wo_b = [B(f"wo{i}") for i in range(2)]
    wo_st = [kb.stream(f"wo{i}") for i in range(2)]
    sg = [carve(512, F32) for _ in range(2)]
    sg_b = [B(f"sg{i}") for i in range(2)]
    stg = [wo[i][:, 0:2 * D].bitcast(F32) for i in range(2)]
    stg_b = wo_b
    stg_st = [kb.stream(f"stg{i}") for i in range(2)]
    aoff[0] = 0
    KT = carve(4 * T).rearrange("p (c t) -> p c t", c=4)
    KT_b = B("KT")
    VX = carve(8 * 17 * 65).rearrange("p (h b e) -> p h b e", h=8, b=17)
    VX_b = B("VX")
    KS = carve(T)
    KS_b = B("KS")
    VSX = carve(2 * 17 * 65).rearrange("p (h b e) -> p h b e", h=2, b=17)
    VSX_b = B("VSX")
    KSb = carve(128)
    VSb = carve(130).rearrange("p (h e) -> p h e", h=2)
    bnd_b = B("bnd")
    pK = [carve(2048) for _ in range(2)]
    pV = [carve(1040).rearrange("p (b e) -> p b e", b=16) for _ in range(2)]
    pKV_b = [B(f"pKV{i}") for i in range(2)]
    pKV_st = [kb.stream(f"pKV{i}") for i in range(2)]
    QFp = carve(8 * 512).rearrange("p (h n) -> p h n", h=8)
    QF_b = B("QFp")
    QSO = carve(8 * 512)
    QS = QSO[:, 0:2048].rearrange("p (c n) -> p c n", c=4)
    otf = QSO[:, 2048:4096].rearrange("p (q f) -> p q f", q=4)
    yT = QSO.rearrange("p (c n) -> p c n", c=8)
    QS_b, otf_b = B("QS"), B("otf")
    YT_bufs = [QS_b, otf_b]
    PT = [carve(512) for _ in range(3)]
    PT_b = [B(f"PT{i}") for i in range(3)]
    sqh, sqh_b = PT[2], PT_b[2]
    sT = [carve(512, F32) for _ in range(2)]
    sT_b = [B(f"sT{i}") for i in range(2)]
    t1, t1_b = lnv, lnv_b
    ots = carve(4 * 512).rearrange("p (q f) -> p q f", q=4)
    ots_b = B("ots")
    oTf = pK[0].rearrange("p (c n) -> p c n", c=4)
    oTs = pK[1].rearrange("p (c n) -> p c n", c=4)
    oTf_b, oTs_b = pKV_b[0], pKV_b[1]
    lf_all = carve(17 * 8, F32).rearrange("p (b h) -> p b h", b=17)
    lf_b = B("lf")
    zt = carve(8, F32)
    et = carve(8, F32)
    zt_b, et_b = B("zt"), B("et")
    wk_all = carve(33 * 8, F32).rearrange("p (b h) -> p b h", b=33)
    tot_all = carve(33 * 8, F32).rearrange("p (b h) -> p b h", b=33)
    E_all = carve(33 * 8, F32).rearrange("p (b h) -> p b h", b=33)
    WE = carve(33 * 8, F32).rearrange("p (b h) -> p b h", b=33)
    tmp8 = carve(8, F32)
    wk_b, tot_b, E_b, WE_b, tmp8_b = B("wk"), B("tot"), B("E"), B("WE"), B("tmp8")
    wkp_b, totp_b = B("wkp"), B("totp")
    bias_g, biasg_b = tot_all, tot_b
    den = carve(8, F32)
    rcp = carve(8, F32)
    den_b, rcp_b = B("den"), B("rcp")
    xst = [[kb.stream(f"xst{l}_{c}") for c in range(4)] for l in range(DEPTH)]
    ccs = [[kb.stream(f"cc{l}_{c}") for c in range(4)] for l in range(DEPTH)]
    xld = kb.stream("xld")
    xld2 = kb.stream("xld2")
    xd_b = [[B(f"xd{l}_{c}") for c in range(4)] for l in range(DEPTH)]

    pbank = [ps(f"pb{i}") for i in range(8)]
    pbank_b = [B(f"pb{i}") for i in range(8)]
    pbank_bf = [p.bitcast(BF16) for p in pbank]

    kb.dma(sp, cst, lambda e: e.dma_start(out=cmat[:], in_=cmat_d), writes=[cmat_b])
    kb.dma(sp, cst, lambda e: e.dma_start(out=cvec[:], in_=cvec_d), writes=[cvec_b])
    kb.dma(sp, cst, lambda e: e.dma_start(out=swab[:].rearrange("p a h q -> p (a h q)"), in_=swab_d),
           writes=[bias_b])
    kb.dma(sp, cst, lambda e: e.dma_start(out=metab[:].rearrange("p h q -> p (h q)"), in_=metab_d),
           writes=[bias_b])
    kb.dma(sp, cst, lambda e: e.dma_start(out=metaq[:].rearrange("p h q -> p (h q)"), in_=metaq_d),
           writes=[bias_b])
    for b_ in (cmat_b, cvec_b, bias_b):
        b_.lw = (cst, cst.count)
    kb.op(dve, lambda e: e.memset(ones_bf[:], 1.0), writes=[ones_b])
    kb.op(dve, lambda e: e.memset(ones_f[:], 1.0), writes=[ones_b])
    kb.op(dve, lambda e: e.tensor_copy(out=cbf[:], in_=cmat[:]), reads=[cmat_b], writes=[cbf_b])

    def load_block(src_ap, nrows, col0, slot):
        kb.dma(sp, stg_st[slot], lambda e: e.dma_start(out=stg[slot][:nrows, :], in_=src_ap),
               writes=[stg_b[slot]])
        g = [i for i, (c0, n) in enumerate(GROUPS) if c0 <= col0 < c0 + n][0]
        for q in range(2):
            pb = q
            for kk in range(4):
                k = q * 4 + kk
                kb.op(pe, lambda e, k=k, kk=kk, pb=pb: e.transpose(
                    out=pbank[pb][:, kk * nrows:(kk + 1) * nrows],
                    in_=stg[slot][:nrows, k * 128:(k + 1) * 128],
                    identity=ident[:nrows, :nrows]),
                    reads=[stg_b[slot], cmat_b], writes=[pbank_b[pb]], sig=(kk == 3))
            if q == 0:
                fn = lambda e, q=q, pb=pb: e.activation(
                    out=hT[:, q * 4:(q + 1) * 4, col0:col0 + nrows],
                    in_=pbank[pb][:, :4 * nrows].rearrange("p (a n) -> p a n", a=4), func=AF.Copy)
            else:
                fn = lambda e, q=q, pb=pb: e.tensor_copy(
                    out=hT[:, q * 4:(q + 1) * 4, col0:col0 + nrows],
                    in_=pbank[pb][:, :4 * nrows].rearrange("p (a n) -> p a n", a=4))
            kb.op(act if q == 0 else dve, fn, reads=[pbank_b[pb]],
                  writes=[hT_b[k][g] for k in range(q * 4, q * 4 + 4)])

    def load_x(blks):
        for blk in blks:
            load_block(x_d[blk * 128:(blk + 1) * 128, :], 128, NMETA + blk * 128, (blk + 1) % 2)

    load_block(meta_d, NMETA, 0, 0)
    load_x(range(8))

    wt_rr = [0]

    class WB:
        pass

    def load_w(parts, kdim, cw):
        nh = 1 if kdim * cw <= 2048 else 2
        i = wt_rr[0] % NH
        if nh == 2 and i % 2 == 1:
            i = (i + 1) % NH
            wt_rr[0] += 1
        wt_rr[0] += nh
        bufs = [wt_b[i + d_] for d_ in range(nh)]
        st_ = wt_st[i]
        view = wt_all[:, i * 2048:i * 2048 + kdim * cw].rearrange("p (k n) -> p k n", k=kdim)
        for n_, (src3, coff, w_) in enumerate(parts):
            kb.dma(pool, st_, lambda e, src3=src3, coff=coff, w_=w_: e.dma_start(
                out=view[:, :src3.shape[1], coff:coff + w_], in_=src3),
                writes=bufs if n_ == 0 else [])
        for b_ in bufs:
            b_.lw = (st_, st_.count)
        return view, bufs

    wo_rr = [0]

    def load_wo(src3, kdim, cw):
        i = wo_rr[0] % 2
        wo_rr[0] += 1
        view = wo[i][:, :kdim * cw].rearrange("p (k n) -> p k n", k=kdim)
        kb.dma(pool, wo_st[i], lambda e: e.dma_start(out=view, in_=src3), writes=[wo_b[i]])
        return view, wo_b[i]

    def rmsnorm(g, gcol, hn=None, hn_b=None):
        if hn is None:
            hn, hn_b = hn_g, hn_gb
        c0, n = GROUPS[g]
        kb.op(act, lambda e: e.activation(out=hn[:, :, :n], in_=hT[:, :, c0:c0 + n], func=AF.Square),
              reads=[hT_b[k][g] for k in range(KC)], writes=[hn_b])
        pb = 2
        for k in range(KC):
            kb.op(pe, lambda e, k=k: e.matmul(out=pbank[pb][:, :n], lhsT=ones_bf[:], rhs=hn[:, k, :n],
                                              start=(k == 0), stop=(k == KC - 1)),
                  reads=[hn_b, ones_b], writes=[pbank_b[pb]], sig=(k == KC - 1))
        kb.op(act, lambda e: e.activation(out=lnv[:, :n], in_=pbank[pb][:, :n], func=AF.Ln,
                                          scale=1.0 / D, bias=EPS),
              reads=[pbank_b[pb]], writes=[lnv_b])
        kb.op(act, lambda e: e.activation(out=rstd[:, :n], in_=lnv[:, :n], func=AF.Exp, scale=-0.5),
              reads=[lnv_b], writes=[rstd_b])
        for k in range(KC):
            kb.op(dve, lambda e, k=k: e.scalar_tensor_tensor(
                out=hn[:, k, :n], in0=hT[:, k, c0:c0 + n], scalar=cvec[:, gcol + k:gcol + k + 1],
                in1=rstd[:, :n], op0=ALU.mult, op1=ALU.mult),
                reads=[hT_b[k][g], cvec_b, rstd_b], writes=[hn_b])

    def headnorm(pb, n, gcol, out_ap, out_bufs, halves=None):
        kb.op(act, lambda e: e.activation(out=sqh[:, :n], in_=pbank[pb][:, :n], func=AF.Square),
              reads=[pbank_b[pb]], writes=[sqh_b])
        kb.op(pe, lambda e: e.matmul(out=pbank[7][:, :n], lhsT=bd_bf, rhs=sqh[:, :n], start=True, stop=True),
              reads=[sqh_b, cbf_b], writes=[pbank_b[7]])
        kb.op(act, lambda e: e.activation(out=lnv[:, :n], in_=pbank[7][:, :n], func=AF.Ln,
                                          scale=1.0 / 64, bias=EPS),
              reads=[pbank_b[7]], writes=[lnv_b])
        kb.op(act, lambda e: e.activation(out=rstd[:, :n], in_=lnv[:, :n], func=AF.Exp, scale=-0.5),
              reads=[lnv_b], writes=[rstd_b])
        if halves is None:
            kb.op(dve, lambda e: e.scalar_tensor_tensor(
                out=out_ap, in0=pbank[pb][:, :n], scalar=cvec[:, gcol:gcol + 1], in1=rstd[:, :n],
                op0=ALU.mult, op1=ALU.mult),
                reads=[pbank_b[pb], cvec_b, rstd_b], writes=out_bufs)
        else:
            for hi, oap in enumerate(halves):
                ps_ = slice(hi * 64, (hi + 1) * 64)
                kb.op(dve, lambda e, oap=oap, ps_=ps_: e.scalar_tensor_tensor(
                    out=oap, in0=pbank[pb][ps_, :n], scalar=cvec[ps_, gcol:gcol + 1], in1=rstd[ps_, :n],
                    op0=ALU.mult, op1=ALU.mult),
                    reads=[pbank_b[pb], cvec_b, rstd_b], writes=out_bufs)

    def chain(items, banks=(3, 4, 5, 6)):
        prev = None
        for i, (pf_, nf_) in enumerate(items):
            pb = banks[i % len(banks)]
            pf_(pb)
            if prev is not None:
                prev()
            prev = (lambda pb=pb, nf_=nf_: nf_(pb))
        if prev is not None:
            prev()

    def proj_fm(wv, wv_b, kdim, mcol, rhs_fn, rhs_bufs, pb, n):
        for k in range(kdim):
            kb.op(pe, lambda e, k=k: e.matmul(out=pbank[pb][:, :n], lhsT=wv[:, k, mcol:mcol + 128],
                                              rhs=rhs_fn(k), start=(k == 0), stop=(k == kdim - 1)),
                  reads=list(wv_b) + rhs_bufs, writes=[pbank_b[pb]], sig=(k == kdim - 1))

    def ffn_views(groups):
        offs = []
        o = 0
        for g in groups:
            offs.append(o)
            o += GROUPS[g][1]
        return offs, [hnF[:, :, offs[gi]:offs[gi] + GROUPS[g][1]] for gi, g in enumerate(groups)]

    def ffn_norms(groups, gcol):
        _, views = ffn_views(groups)
        for gi, g in enumerate(groups):
            rmsnorm(g, gcol, views[gi], hnF_b[gi])

    def ffn(groups, gcol, w_in_l, w_out_l, pre_normed=False, hoist=None, hoist0=None):
        offs, views = ffn_views(groups)
        w_in3 = w_in_l.rearrange("(k p) n -> p k n", p=128)
        tiles = [(i * 256, 256) for i in range(11)]
        itc = [0]

        def gate_up(wg, wg_b, wu, wu_b, h0, jj, gi):
            g = groups[gi]
            j = h0 // 128 + jj
            n = GROUPS[g][1]
            v = views[gi]
            it = itc[0]
            itc[0] += 1
            pg, pu = 3 + 2 * (it % 2), 4 + 2 * (it % 2)
            s_ = it % 2
            proj_fm(wg, wg_b, KC, jj * 128, lambda k: v[:, k, :], [hnF_b[gi]], pg, n)
            proj_fm(wu, wu_b, KC, jj * 128, lambda k: v[:, k, :], [hnF_b[gi]], pu, n)
            kb.op(act, lambda e: e.activation(out=sg[s_][:, :n], in_=pbank[pg][:, :n], func=AF.Silu),
                  reads=[pbank_b[pg]], writes=[sg_b[s_]])
            kb.op(dve, lambda e: e.tensor_tensor(
                out=aT[:, j, offs[gi]:offs[gi] + n], in0=pbank[pu][:, :n], in1=sg[s_][:, :n], op=ALU.mult),
                reads=[pbank_b[pu], sg_b[s_]], writes=[aT_b[j][gi]])

        for ti_, (h0, hw) in enumerate(tiles):
            ng = len(groups)
            if ti_ == 0 and not pre_normed:
                rmsnorm(groups[0], gcol, views[0], hnF_b[0])
                if ng > 1:
                    rmsnorm(groups[1], gcol, views[1], hnF_b[1])
            wg, wg_b = load_w([(w_in3[:, :, h0:h0 + hw], 0, hw)], KC, hw)
            wu, wu_b = load_w([(w_in3[:, :, DFF + h0:DFF + h0 + hw], 0, hw)], KC, hw)
            if ti_ == 0:
                for gi in range(ng):
                    for jj in range(hw // 128):
                        gate_up(wg, wg_b, wu, wu_b, h0, jj, gi)
                    if gi + 2 < ng and not pre_normed:
                        rmsnorm(groups[gi + 2], gcol, views[gi + 2], hnF_b[gi + 2])
            else:
                for jj in range(hw // 128):
                    for gi in range(len(groups)):
                        gate_up(wg, wg_b, wu, wu_b, h0, jj, gi)
            if ti_ == 0 and hoist0 is not None:
                hoist0()
        w_out3 = w_out_l.rearrange("(j p) n -> p j n", p=128)
        it = 0
        for mt in range(4):
            wv, wv_b = load_wo(w_out3[:, :, mt * 256:(mt + 1) * 256], JC, 256)
            if mt == 0 and hoist is not None:
                hoist()
            for mm in range(2):
                m = mt * 2 + mm
                for gi, g in enumerate(groups):
                    c0, n = GROUPS[g]
                    o_ = offs[gi]
                    pb = it % 2
                    it += 1
                    for j in range(JC):
                        kb.op(pe, lambda e, j=j, mm=mm, pb=pb, wv=wv, n=n, o_=o_: e.matmul(
                            out=pbank[pb][:, :n], lhsT=wv[:, j, mm * 128:(mm + 1) * 128], rhs=aT[:, j, o_:o_ + n],
                            start=(j == 0), stop=(j == JC - 1)),
                            reads=[wv_b, aT_b[j][gi]], writes=[pbank_b[pb]], sig=(j == JC - 1))
                    kb.op(dve, lambda e, m=m, pb=pb, c0=c0, n=n: e.scalar_tensor_tensor(
                        out=hT[:, m, c0:c0 + n], in0=pbank[pb][:, :n], scalar=0.5, in1=hT[:, m, c0:c0 + n],
                        op0=ALU.mult, op1=ALU.add),
                        reads=[pbank_b[pb], hT_b[m][g]], writes=[hT_b[m][g]])

    def mixer_kv(l):
        cb = CVL * l
        w3 = win_d[l].rearrange("(k p) n -> p k n", p=128)
        kb.op(dve, lambda e: e.memset(VX[:, :, :, 64:65], 1.0), writes=[VX_b])
        kb.op(dve, lambda e: e.memset(VX[:, :, 0, :], 0.0), writes=[VX_b])
        kb.op(dve, lambda e: e.memset(VX[:16, :, 0, 64:65], 1.0), writes=[VX_b])
        kb.op(dve, lambda e: e.memset(VSX[:, :, :, 64:65], 1.0), writes=[VSX_b])
        kb.op(dve, lambda e: e.memset(lf_all[:, 0, :], 0.0), writes=[lf_b])
        kb.op(act, lambda e: e.activation(out=esink[:], in_=cvec[:, cb + 36:cb + 44], func=AF.Exp),
              reads=[cvec_b], writes=[esink_b])
        hn_alt = QFp.rearrange("p h n -> p (h n)").rearrange("p (k n) -> p k n", k=KC)
        hbufs = [(hn_g, hn_gb), (hn_alt, QF_b)]

        def kv_group(g):
            c0, n = GROUPS[g]
            hn, hn_b = hbufs[g % 2]
            if g == 0:
                rmsnorm(g, cb + 8, hn, hn_b)
            wk_, wkb_ = load_w([(w3[:, :, 512:1024], 0, 512)], KC, 512)
            wks_, wksb_ = load_w([(w3[:, :, 2056:2184], 0, 128)], KC, 128)
            items = []
            for mc in range(4):
                items.append((
                    lambda pb, mc=mc: proj_fm(wk_, wkb_, KC, mc * 128, lambda k: hn[:, k, :n], [hn_b], pb, n),
                    lambda pb, mc=mc: headnorm(pb, n, cb + 25, KT[:, mc, c0:c0 + n], [KT_b])))
            items.append((
                lambda pb: proj_fm(wks_, wksb_, KC, 0, lambda k: hn[:, k, :n], [hn_b], pb, n),
                lambda pb: headnorm(pb, n, cb + 27, KS[:, c0:c0 + n], [KS_b])))
            chain(items)
            if g + 1 < 5:
                rmsnorm(g + 1, cb + 8, *hbufs[(g + 1) % 2])
            wv_, wvb_ = load_w([(w3[:, :, 1024:1536], 0, 512)], KC, 512)
            if "f" in SKIP:
                return
            wf_, wfb_ = load_w([(wvsf_d[l].rearrange("(k p) n -> p k n", p=128), 0, 136)], KC, 136)
            nblk = max(1, n // 128)
            for bi in range(nblk):
                rows = min(n, 128)
                lb = 0 if g == 0 else 4 * (g - 1) + 1 + bi
                cs = bi * 128
                pv, pf = bi % 2, 2
                for k in range(KC):
                    kb.op(pe, lambda e, k=k, pv=pv, cs=cs, rows=rows: e.matmul(
                        out=pbank[pv][:rows, :512], lhsT=hn[:, k, cs:cs + rows], rhs=wv_[:, k, :],
                        start=(k == 0), stop=(k == KC - 1)),
                        reads=[hn_b] + wvb_, writes=[pbank_b[pv]], sig=(k == KC - 1))
                if "f" in SKIP:
                    continue
                kb.op(act, lambda e, pv=pv, rows=rows, lb=lb: e.activation(
                    out=VX[:rows, :, lb, 0:64], in_=pbank[pv][:rows, :512].rearrange("p (h d) -> p h d", h=8),
                    func=AF.Copy), reads=[pbank_b[pv]], writes=[VX_b])
                for k in range(KC):
                    kb.op(pe, lambda e, k=k, pf=pf, cs=cs, rows=rows: e.matmul(
                        out=pbank[pf][:rows, :136], lhsT=hn[:, k, cs:cs + rows], rhs=wf_[:, k, :],
                        start=(k == 0), stop=(k == KC - 1)),
                        reads=[hn_b] + wfb_, writes=[pbank_b[pf]], sig=(k == KC - 1))
                kb.op(act, lambda e, pf=pf, rows=rows, lb=lb: e.activation(
                    out=VSX[:rows, :, lb, 0:64], in_=pbank[pf][:rows, :128].rearrange("p (h d) -> p h d", h=2),
                    func=AF.Copy), reads=[pbank_b[pf]], writes=[VSX_b])
                if "a" in SKIP:
                    continue
                kb.op(act, lambda e, pf=pf, rows=rows: e.activation(
                    out=zt[:rows, :], in_=pbank[pf][:rows, 128:136], func=AF.Copy),
                    reads=[pbank_b[pf]], writes=[zt_b])
                kb.op(dve, lambda e, rows=rows: e.tensor_tensor(
                    out=et[:rows, :], in0=zt[:rows, :], in1=cvec[:rows, cb + 28:cb + 36], op=ALU.add),
                    reads=[zt_b, cvec_b], writes=[et_b])
                kb.op(act, lambda e, rows=rows: e.activation(out=zt[:rows, :], in_=et[:rows, :], func=AF.Exp,
                                                             scale=-1.0),
                      reads=[et_b], writes=[zt_b])
                kb.op(dve, lambda e, rows=rows: e.tensor_scalar(out=et[:rows, :], in0=zt[:rows, :], scalar1=1.0,
                                                                scalar2=None, op0=ALU.add),
                      reads=[zt_b], writes=[et_b])
                kb.op(act, lambda e, rows=rows, lb=lb: e.activation(out=lf_all[:rows, lb, :], in_=et[:rows, :],
                                                                    func=AF.Ln),
                      reads=[et_b], writes=[lf_b])
        for g in range(5):
            kv_group(g)
        kb.op(dve, lambda e: e.memset(QFp[:, :, :], 0.0), writes=[QF_b])
        if "c" in SKIP:
            return
        lf2 = lf_all[:].rearrange("p b h -> p (b h)") if False else lf_all.rearrange("p b h -> p (b h)")
        kb.op(pe, lambda e: e.matmul(out=pbank[0][:, :136], lhsT=U_f, rhs=lf2, start=True, stop=True),
              reads=[lf_b, cmat_b], writes=[pbank_b[0]])
        kb.op(pe, lambda e: e.matmul(out=pbank[1][:, :136], lhsT=ones_f[:], rhs=lf2, start=True, stop=True),
              reads=[lf_b, ones_b], writes=[pbank_b[1]])
        wk2 = wk_all.rearrange("p b h -> p (b h)")
        tot2 = tot_all.rearrange("p b h -> p (b h)")
        kb.op(dve, lambda e: e.tensor_copy(out=wk2[:, 0:8], in_=pbank[0][:, 0:8]),
              reads=[pbank_b[0]], writes=[wk_b])
        kb.op(dve, lambda e: e.tensor_copy(out=wk2[:, 136:264], in_=pbank[0][:, 8:136]),
              reads=[pbank_b[0]], writes=[wk_b])
        kb.op(act, lambda e: e.activation(out=tot2[:, 0:8], in_=pbank[1][:, 0:8], func=AF.Copy),
              reads=[pbank_b[1]], writes=[tot_b])
        kb.op(act, lambda e: e.activation(out=tot2[:, 136:264], in_=pbank[1][:, 8:136], func=AF.Copy),
              reads=[pbank_b[1]], writes=[tot_b])

    def exchange(l):
        xi = xin_d[l]
        def st(c, fn, reads):
            kb.dma(sp, xst[l][c], fn, reads=reads, writes=[])
        st(0, lambda e: e.dma_start(out=xi[0][:, :].rearrange("p (c t) -> p c t", c=4), in_=KT[:, :, NMETA:T]),
           [KT_b])
        st(1, lambda e: e.dma_start(out=xi[1][:, :].rearrange("p (h x) -> p h x", h=4),
                                    in_=VX[:, 0:4, 1:17, :].rearrange("p h b e -> p h (b e)")), [VX_b])
        st(2, lambda e: e.dma_start(out=xi[2][:, 0:4160].rearrange("p (h x) -> p h x", h=4),
                                    in_=VX[:, 4:8, 1:17, :].rearrange("p h b e -> p h (b e)")), [VX_b])
        st(3, lambda e: e.dma_start(
            out=xi[3][:, X_WK:X_WK + 128], in_=wk_all[:, 17:33, :].rearrange("p b h -> p (b h)")), [wk_b])
        st(3, lambda e: e.dma_start(
            out=xi[3][:, X_TOT:X_TOT + 128], in_=tot_all[:, 17:33, :].rearrange("p b h -> p (b h)")), [tot_b])
        st(2, lambda e: e.dma_start(out=xi[2][:, X_KSB:X_KSB + 128], in_=KS[:, T - 128:T]), [KS_b])
        st(2, lambda e: e.dma_start(
            out=xi[2][:, X_VSB:X_VSB + 130].rearrange("p (h e) -> p h e", h=2), in_=VSX[:, :, 16, :]), [VSX_b])
        for c in (3, 0, 1, 2):
            xd_b[l][c].lw = (xst[l][c], xst[l][c].count)

    def exchange_cc(l, chunks):
        xi = xin_d[l]
        for c in chunks:
            kb._deps(pool, [xd_b[l][c]], [])
            pool.ops.append(("cc", lambda e, c=c: e.collective_compute(
                "AllGather", ALU.bypass, replica_groups=[[0, 1], [2, 3], [4, 5], [6, 7]],
                ins=[xi[c]], outs=[xout_d[l][c]]), ccs[l][c]))
            ccs[l][c].count += 1
            xd_b[l][c].lw = (ccs[l][c], 1)

    def exchange_post(l):
        xo3 = xout_d[l][3]
        kb.dma(sp, xld, lambda e: e.dma_start(
            out=wk_all[:, 1:17, :].rearrange("p b h -> p (b h)"), in_=xo3[0:128, X_WK:X_WK + 128]),
            reads=[xd_b[l][3]], writes=[wkp_b])
        kb.dma(sp, xld, lambda e: e.dma_start(
            out=tot_all[:, 1:17, :].rearrange("p b h -> p (b h)"), in_=xo3[0:128, X_TOT:X_TOT + 128]),
            reads=[xd_b[l][3]], writes=[totp_b])
        for b_ in (wkp_b, totp_b):
            b_.lw = (xld, xld.count)

    def load_bnd(l):
        xo = xout_d[l][2]
        kb.dma(sp, xld2, lambda e: e.dma_start(out=KSb, in_=xo[0:128, X_KSB:X_KSB + 128]),
               reads=[xd_b[l][2]], writes=[bnd_b])
        kb.dma(sp, xld2, lambda e: e.dma_start(
            out=VSb, in_=xo[0:128, X_VSB:X_VSB + 130].rearrange("p (h e) -> p h e", h=2)),
            reads=[xd_b[l][2]], writes=[bnd_b])
        bnd_b.lw = (xld2, xld2.count)

    def exchange_E(l):
        TT = ALU
        kb.op(dve, lambda e: e.memset(E_all[:, 0, :], 0.0), writes=[E_b])
        kb.op(dve, lambda e: e.tensor_copy(out=E_all[:, 1, :], in_=tot_all[:, 0, :]), reads=[tot_b], writes=[E_b])
        for j in range(2, 17):
            kb.op(dve, lambda e, j=j: e.tensor_tensor(out=E_all[:, j, :], in0=E_all[:, j - 1, :],
                                                      in1=tot_all[:, j - 1, :], op=TT.add),
                  reads=[E_b, tot_b, totp_b], writes=[E_b])
        kb.op(dve, lambda e: e.tensor_tensor(out=tmp8[:, :], in0=E_all[:, 16, :], in1=tot_all[:, 16, :], op=TT.add),
              reads=[E_b, tot_b, totp_b], writes=[tmp8_b])
        kb.op(dve, lambda e: e.tensor_tensor(out=tmp8[:, :], in0=tmp8[:, :], in1=tot_all[:, 0, :], op=TT.subtract),
              reads=[tmp8_b, tot_b], writes=[tmp8_b])
        kb.op(dve, lambda e: e.scalar_tensor_tensor(
            out=E_all[:, 17, :], in0=tmp8[:, :], scalar=cvec[:, CV_FLAG:CV_FLAG + 1], in1=tot_all[:, 0, :],
            op0=TT.mult, op1=TT.add), reads=[tmp8_b, cvec_b, tot_b], writes=[E_b])
        for j in range(18, 33):
            kb.op(dve, lambda e, j=j: e.tensor_tensor(out=E_all[:, j, :], in0=E_all[:, j - 1, :],
                                                      in1=tot_all[:, j - 1, :], op=TT.add),
                  reads=[E_b, tot_b, totp_b], writes=[E_b])
        kb.op(dve, lambda e: e.tensor_scalar(
            out=E_all[:, 1:17, :], in0=E_all[:, 1:17, :], scalar1=cvec[:, CV_MASK:CV_MASK + 1], scalar2=None,
            op0=TT.add), reads=[E_b, cvec_b], writes=[E_b])
        kb.op(dve, lambda e: e.tensor_tensor(out=WE[:], in0=wk_all[:], in1=E_all[:], op=TT.add),
              reads=[wk_b, wkp_b, E_b], writes=[WE_b])

    def mixer_pre(l, g):
        cb = CVL * l
        c0, n = GROUPS[g]
        w3 = win_d[l].rearrange("(k p) n -> p k n", p=128)
        rmsnorm(g, cb + 8)
        wq_, wqb_ = load_w([(w3[:, :, 0:512], 0, 512)], KC, 512)
        wq2_, wq2b_ = load_w([(w3[:, :, 1544:2056], 0, 512)], KC, 512)
        items = []
        for mc in range(4):
            items.append((
                lambda pb, mc=mc: proj_fm(wq_, wqb_, KC, mc * 128, lambda k: hn[:, k, :n], [hn_b], pb, n),
                lambda pb, mc=mc: headnorm(pb, n, cb + 24, None, [QF_b],
                                           halves=[QFp[0:64, 2 * mc, :n], QFp[64:128, 2 * mc + 1, :n]])))
        for mc in range(4):
            items.append((
                lambda pb, mc=mc: proj_fm(wq2_, wq2b_, KC, mc * 128, lambda k: hn[:, k, :n], [hn_b], pb, n),
                lambda pb, mc=mc: headnorm(pb, n, cb + 26, QS[:, mc, :n], [QS_b])))
        chain(items)

    def mixer_out(l, g):
        cb = CVL * l
        c0, n = GROUPS[g]
        w3 = win_d[l].rearrange("(k p) n -> p k n", p=128)

        nq = max(1, n // 128)
        rq = min(n, 128)
        if g > 0:
            eref = 16 + 4 * (g - 1) + 1 + 2
            kb.op(dve, lambda e: e.tensor_tensor(
                out=bias_g[:], in0=WE[:], in1=E_all[:, eref:eref + 1, :].broadcast_to([128, 33, 8]),
                op=ALU.subtract), reads=[WE_b, E_b], writes=[biasg_b, totp_b])
        itile = [0]
        stages = []

        def emit_load(h):
            ch, p0 = h // 2, (h % 2) * 64
            s = (g * 8 + h) % 2
            kb.dma(sp, pKV_st[s], lambda e: e.dma_start(
                out=pK[s][:, :], in_=xout_d[l][0][0:128, ch * 2048:(ch + 1) * 2048]),
                reads=[xd_b[l][0]], writes=[pKV_b[s]])
            vc = 1 + h // 4
            kb.dma(sp, pKV_st[s], lambda e: e.dma_start(
                out=pV[s][:].rearrange("p b e -> p (b e)"),
                in_=xout_d[l][vc][0:128, (h % 4) * 1040:(h % 4 + 1) * 1040]),
                reads=[xd_b[l][vc]])
            pKV_b[s].lw = (pKV_st[s], pKV_st[s].count)

        def fox_head(h):
            ch, p0 = h // 2, (h % 2) * 64
            ob = 5 + h % 2
            tiles = []
            if g == 0:
                tiles.append((KT[:, ch, 0:16], [KT_b], VX[:16, h, 0, :], [VX_b],
                              wk_all[:16, 0, h:h + 1], [wk_b], 16, 0, 0))
            else:
                s = (g * 8 + h) % 2
                tiles.append((KT[:, ch, 0:128], [KT_b], VX[:, h, 0, :], [VX_b],
                              bias_g[:, 0, h:h + 1], [biasg_b], 128, 0, None))
                for jb in range(16):
                    tiles.append((pK[s][:, jb * 128:(jb + 1) * 128], [pKV_b[s]], pV[s][:, jb, :],
                                  [pKV_b[s]], bias_g[:, 1 + jb, h:h + 1], [biasg_b], 128, 0, None))
                for lb in range(1, 4 * g + 1):
                    d = lb - (4 * (g - 1) + 1)
                    kc0 = NMETA + (lb - 1) * 128
                    tiles.append((KT[:, ch, kc0:kc0 + 128], [KT_b], VX[:, h, lb, :], [VX_b],
                                  bias_g[:, 16 + lb, h:h + 1], [biasg_b], 128,
                                  max(d, 0) * 128, d if d >= 0 else None))
            slots = {}
            ntl = len(tiles)

            def mk(ti, tile):
                ksrc, kbufs, vsrc, vbufs, bsrc, bbufs, rows, q0, diag = tile

                def front():
                    if g > 0:
                        if h == 0 and ti == 0:
                            emit_load(0)
                        if ti == 2 and h + 1 < 8:
                            emit_load(h + 1)
                    it = itile[0]
                    itile[0] += 1
                    sb_, ps_ = it % 3, (3, 4, 7)[it % 3]
                    slots[ti] = sb_
                    kb.op(pe, lambda e: e.matmul(
                        out=pbank[ps_][:rows, q0:n], lhsT=ksrc, rhs=QFp[:, h, q0:n], start=True, stop=True),
                        reads=kbufs + [QF_b], writes=[pbank_b[ps_]])
                    kb.op(act, lambda e: e.activation(
                        out=PT[sb_][:rows, q0:n], in_=pbank[ps_][:rows, q0:n], func=AF.Exp, scale=SCALE, bias=bsrc),
                        reads=[pbank_b[ps_]] + bbufs, writes=[PT_b[sb_]])
                    if diag is not None:
                        kb.op(dve, lambda e: e.tensor_tensor(
                            out=PT[sb_][:rows, q0:q0 + rq], in0=PT[sb_][:rows, q0:q0 + rq], in1=tri_bf[:rows, :rq],
                            op=ALU.mult), reads=[PT_b[sb_], cbf_b], writes=[PT_b[sb_]])

                def back():
                    sb_ = slots[ti]
                    last_tile = (ti == ntl - 1)
                    for qb in range(q0 // 128, nq):
                        first = (ti == 0 and qb == q0 // 128)
                        kb.op(pe, lambda e, qb=qb, first=first: e.matmul(
                            out=pbank[ob][:rq, qb * 65:(qb + 1) * 65], lhsT=PT[sb_][:rows, qb * 128:qb * 128 + rq],
                            rhs=vsrc, start=first, stop=(last_tile and qb == nq - 1)),
                            reads=[PT_b[sb_]] + vbufs, writes=[pbank_b[ob]], sig=(qb == nq - 1))
                    if last_tile:
                        O3 = pbank[ob][:rq, :nq * 65].rearrange("p (q e) -> p q e", q=nq)
                        kb.op(dve, lambda e: e.tensor_scalar(out=den[:rq, :nq], in0=O3[:, :, 64], scalar1=1e-30,
                                                             scalar2=None, op0=ALU.max),
                              reads=[pbank_b[ob]], writes=[den_b])
                        kb.op(dve, lambda e: e.reciprocal(out=rcp[:rq, :nq], in_=den[:rq, :nq]),
                              reads=[den_b], writes=[rcp_b])
                        kb.op(dve, lambda e: e.tensor_tensor(
                            out=otf[:rq, :nq, h * 64:(h + 1) * 64], in0=O3[:, :, 0:64],
                            in1=rcp[:rq, :nq].unsqueeze(2).broadcast_to([rq, nq, 64]), op=ALU.mult),
                            reads=[pbank_b[ob], rcp_b], writes=[otf_b])
                return front, back

            for ti, tile in enumerate(tiles):
                stages.append(mk(ti, tile))

        for h in range(8):
            fox_head(h)

        def swa_blk(qb, g2):
            lb = 0 if g == 0 else 4 * (g - 1) + 1 + qb
            p0 = g2 * 64
            ob = 5 + g2
            tiles = []
            if g == 0:
                tiles.append((KS[p0:p0 + 64, 0:16], [KS_b], VSX[:16, g2, 0, :], [VSX_b],
                              metaq[:16, g2 * 4:(g2 + 1) * 4, :], 16))
            else:
                kc0 = NMETA + (lb - 1) * 128
                if lb == 1:
                    tiles.append((KSb[p0:p0 + 64, :], [bnd_b], VSb[:, g2, :], [bnd_b],
                                  swab[:, 0, g2 * 4:(g2 + 1) * 4, :], 128))
                else:
                    tiles.append((KS[p0:p0 + 64, kc0 - 128:kc0], [KS_b], VSX[:, g2, lb - 1, :], [VSX_b],
                                  swab[:, 0, g2 * 4:(g2 + 1) * 4, :], 128))
                tiles.append((KS[p0:p0 + 64, kc0:kc0 + 128], [KS_b], VSX[:, g2, lb, :], [VSX_b],
                              swab[:, 1, g2 * 4:(g2 + 1) * 4, :], 128))
                tiles.append((KS[p0:p0 + 64, 0:16], [KS_b], VSX[:16, g2, 0, :], [VSX_b],
                              metab[:16, g2 * 4:(g2 + 1) * 4, 0:128] if lb == 1 else
                              metab[:16, g2 * 4:(g2 + 1) * 4, 128:129].broadcast_to([16, 4, 128]), 16))
            slots = {}
            ntl = len(tiles)

            def mk(ti, tile):
                ksrc, kbufs, vsrc, vbufs, bsrc, rows = tile

                def front():
                    if g == 1 and lb == 1 and g2 == 0 and ti == 0:
                        load_bnd(l)
                    it = itile[0]
                    itile[0] += 1
                    sb_, ps_, ss_ = it % 3, (3, 4, 7)[it % 3], it % 2
                    slots[ti] = sb_
                    for r in range(4):
                        kb.op(pe, lambda e, r=r: e.matmul(
                            out=pbank[ps_][:rows, r * rq:(r + 1) * rq], lhsT=ksrc,
                            rhs=QS[p0:p0 + 64, r, qb * 128:qb * 128 + rq], start=True, stop=True),
                            reads=kbufs + [QS_b], writes=[pbank_b[ps_]], sig=(r == 3))
                    kb.op(dve, lambda e: e.scalar_tensor_tensor(
                        out=sT[ss_][:rows, :4 * rq].rearrange("p (r q) -> p r q", r=4),
                        in0=pbank[ps_][:rows, :4 * rq].rearrange("p (r q) -> p r q", r=4), scalar=SCALE,
                        in1=bsrc, op0=ALU.mult, op1=ALU.add),
                        reads=[pbank_b[ps_], bias_b], writes=[sT_b[ss_]])
                    if g > 0 and lb == 1 and ti == 0:
                        kb.op(dve, lambda e: e.tensor_scalar(
                            out=sT[ss_][:, :512], in0=sT[ss_][:, :512], scalar1=cvec[:, CV_MASK:CV_MASK + 1],
                            scalar2=None, op0=ALU.add), reads=[sT_b[ss_], cvec_b], writes=[sT_b[ss_]])
                    kb.op(act, lambda e: e.activation(
                        out=PT[sb_][:rows, :4 * rq], in_=sT[ss_][:rows, :4 * rq], func=AF.Exp),
                        reads=[sT_b[ss_]], writes=[PT_b[sb_]])

                def back():
                    sb_ = slots[ti]
                    last_tile = (ti == ntl - 1)
                    for r in range(4):
                        first = (ti == 0 and r == 0)
                        kb.op(pe, lambda e, r=r, first=first: e.matmul(
                            out=pbank[ob][:rq, r * 65:(r + 1) * 65], lhsT=PT[sb_][:rows, r * rq:(r + 1) * rq],
                            rhs=vsrc, start=first, stop=(last_tile and r == 3)),
                            reads=[PT_b[sb_]] + vbufs, writes=[pbank_b[ob]], sig=(r == 3))
                    if last_tile:
                        O3 = pbank[ob][:rq, :260].rearrange("p (q e) -> p q e", q=4)
                        kb.op(dve, lambda e: e.tensor_tensor(out=den[:rq, :4], in0=O3[:, :, 64],
                                                             in1=esink[:rq, g2 * 4:(g2 + 1) * 4], op=ALU.add),
                              reads=[pbank_b[ob], esink_b], writes=[den_b])
                        kb.op(dve, lambda e: e.reciprocal(out=rcp[:rq, :4], in_=den[:rq, :4]),
                              reads=[den_b], writes=[rcp_b])
                        kb.op(dve, lambda e: e.tensor_tensor(
                            out=ots[:rq, qb, g2 * 256:(g2 + 1) * 256].rearrange("p (r d) -> p r d", r=4),
                            in0=O3[:, :, 0:64], in1=rcp[:rq, :4].unsqueeze(2).broadcast_to([rq, 4, 64]),
                            op=ALU.mult), reads=[pbank_b[ob], rcp_b], writes=[ots_b])
                return front, back

            for ti, tile in enumerate(tiles):
                stages.append(mk(ti, tile))

        for qb in range(nq):
            for g2 in range(2):
                swa_blk(qb, g2)

        DEPTH_P = 2
        for t_ in range(len(stages) + DEPTH_P):
            if t_ < len(stages):
                stages[t_][0]()
            if t_ - DEPTH_P >= 0:
                stages[t_ - DEPTH_P][1]()

        for (src, src_b, dst, dst_b) in ((otf, otf_b, oTf, oTf_b), (ots, ots_b, oTs, oTs_b)):
            for qb in range(nq):
                pb = qb % 2
                for c in range(4):
                    kb.op(pe, lambda e, src=src, qb=qb, c=c, pb=pb: e.transpose(
                        out=pbank_bf[pb][:, c * 128:c * 128 + rq], in_=src[:rq, qb, c * 128:(c + 1) * 128],
                        identity=ident_bf[:rq, :rq]),
                        reads=[src_b, cbf_b], writes=[pbank_b[pb]], sig=(c == 3))
                kb.op(act, lambda e, dst=dst, qb=qb, pb=pb: e.activation(
                    out=dst[:, :, qb * 128:qb * 128 + rq],
                    in_=pbank_bf[pb][:, :512].rearrange("p (c q) -> p c q", c=4)[:, :, :rq], func=AF.Copy),
                    reads=[pbank_b[pb]], writes=[dst_b])

        wbf3 = wbf_d[l].rearrange("(k p) n -> p k n", p=128)
        wbs3 = wbs_d[l].rearrange("(k p) n -> p k n", p=128)
        for mt in range(4):
            ca = 2312 + mt * 256
            wA, wA_b = load_w([(w3[:, :, ca:ca + 256], 0, 256), (w3[:, :, ca + 1024:ca + 1280], 256, 256)], KC, 512)
            wB, wB_b = load_w([(wbf3[:, :, mt * 256:(mt + 1) * 256], 0, 256),
                               (wbs3[:, :, mt * 256:(mt + 1) * 256], 256, 256)], 4, 512)
            for mm in range(2):
                m = mt * 2 + mm
                b0, b1, b2, b3 = (3, 4, 5, 6) if m % 2 == 0 else (7, 2, 0, 1)
                proj_fm(wA, wA_b, KC, mm * 128, lambda k: hn[:, k, :n], [hn_b], b0, n)
                proj_fm(wA, wA_b, KC, 256 + mm * 128, lambda k: hn[:, k, :n], [hn_b], b1, n)
                proj_fm(wB, wB_b, 4, mm * 128, lambda k: oTf[:, k, :n], [oTf_b], b2, n)
                proj_fm(wB, wB_b, 4, 256 + mm * 128, lambda k: oTs[:, k, :n], [oTs_b], b3, n)
                kb.op(act, lambda e, b0=b0: e.activation(out=sT[0][:, :n], in_=pbank[b0][:, :n], func=AF.Sigmoid),
                      reads=[pbank_b[b0]], writes=[sT_b[0]])
                kb.op(act, lambda e, b1=b1: e.activation(out=sT[1][:, :n], in_=pbank[b1][:, :n], func=AF.Sigmoid),
                      reads=[pbank_b[b1]], writes=[sT_b[1]])
                kb.op(dve, lambda e, b2=b2: e.tensor_tensor(out=t1[:, :n], in0=pbank[b2][:, :n], in1=sT[0][:, :n],
                                                            op=ALU.mult),
                      reads=[pbank_b[b2], sT_b[0]], writes=[t1_b])
                kb.op(dve, lambda e, b3=b3: e.tensor_tensor(out=sT[1][:, :n], in0=pbank[b3][:, :n],
                                                            in1=sT[1][:, :n], op=ALU.mult),
                      reads=[pbank_b[b3], sT_b[1]], writes=[sT_b[1]])
                kb.op(dve, lambda e, m=m: e.tensor_tensor(out=yT[:, m, :n], in0=t1[:, :n], in1=sT[1][:, :n],
                                                          op=ALU.add),
                      reads=[t1_b, sT_b[1]], writes=YT_bufs)
        wo3 = wout_d[l].rearrange("(k p) n -> p k n", p=128)
        for mt in range(2):
            wO, wO_b = load_w([(wo3[:, :, mt * 512:(mt + 1) * 512], 0, 512)], KC, 512)
            for mm in range(4):
                m = mt * 4 + mm
                pb = mm % 2
                proj_fm(wO, wO_b, KC, mm * 128, lambda k: yT[:, k, :n], YT_bufs, pb, n)
                kb.op(dve, lambda e, m=m, pb=pb: e.tensor_tensor(
                    out=hT[:, m, c0:c0 + n], in0=pbank[pb][:, :n], in1=hT[:, m, c0:c0 + n], op=ALU.add),
                    reads=[pbank_b[pb], hT_b[m][g]], writes=[hT_b[m][g]])

    ost = [kb.stream(f"ost{i}") for i in range(2)]

    def store_blocks(blks):
        for blk in blks:
            slot = blk % 2
            col0 = NMETA + blk * 128
            g = 1 + blk // 4
            for q in range(2):
                pb = q
                for kk in range(4):
                    k = q * 4 + kk
                    kb.op(pe, lambda e, k=k, kk=kk, pb=pb, col0=col0: e.transpose(
                        out=pbank[pb][:, kk * 128:(kk + 1) * 128], in_=hT[:, k, col0:col0 + 128],
                        identity=ident),
                        reads=[hT_b[k][g], cmat_b], writes=[pbank_b[pb]], sig=(kk == 3))
                if q == 0:
                    fn = lambda e, q=q, pb=pb, slot=slot: e.activation(
                        out=stg[slot][:, q * 512:(q + 1) * 512], in_=pbank[pb][:, :], func=AF.Copy)
                else:
                    fn = lambda e, q=q, pb=pb, slot=slot: e.tensor_copy(
                        out=stg[slot][:, q * 512:(q + 1) * 512], in_=pbank[pb][:, :])
                kb.op(act if q == 0 else dve, fn, reads=[pbank_b[pb]], writes=[stg_b[slot]])
            kb.dma(sp, ost[slot], lambda e, blk=blk, slot=slot: e.dma_start(
                out=out_d[blk * 128:(blk + 1) * 128, :], in_=stg[slot][:, :]), reads=[stg_b[slot]])

    for l in range(DEPTH):
        if stage >= 1:
            g1c = CVL * l + 0
            h0_ = (lambda: load_x(range(8, 16))) if l == 0 else None
            ffn((1, 2, 0), g1c, f1_in[l], f1_out[l], pre_normed=(l > 0),
                hoist=lambda g1c=g1c: ffn_norms((3, 4), g1c), hoist0=h0_)
            ffn((3, 4), g1c, f1_in[l], f1_out[l], pre_normed=True)
        if stage == 1:
            break
        kb.barrier()
        mixer_kv(l)
        if stage == 1.5:
            break
        exchange(l)
        mixer_pre(l, 0)
        exchange_cc(l, (3, 0))
        exchange_post(l)
        mixer_out(l, 0)
        exchange_cc(l, (1,))
        mixer_pre(l, 1)
        exchange_cc(l, (2,))
        exchange_E(l)
        for g in range(1, 5):
            if g > 1:
                mixer_pre(l, g)
            mixer_out(l, g)
        kb.barrier()
        if stage in (2, 1.8):
            break
        g2c = CVL * l + 16
        ffn((1, 2, 0), g2c, f2_in[l], f2_out[l], hoist=lambda g2c=g2c: ffn_norms((3, 4), g2c))
        nxt = (lambda c_=CVL * (l + 1): ffn_norms((1, 2, 0), c_)) if (l + 1 < DEPTH and stage > 3) else None
        if l == DEPTH - 1 and stage > 3:
            store_blocks(range(0, 8))
        ffn((3, 4), g2c, f2_in[l], f2_out[l], pre_normed=True, hoist=nxt)
        if stage == 3:
            break

    store_blocks(range(8, 16) if stage > 3 else range(16))
    for s in ost:
        kb._wait(sp, s, s.count)

    kb.emit()
    stack.close()
    return nc


def _t5_bucket(d):
    d = np.maximum(d, 0)
    nf = np.maximum(d, 1).astype(np.float32)
    large = 16 + (np.log(nf / np.float32(16)) / np.float32(np.log(128 / 16)) * np.float32(16)).astype(np.int32)
    large = np.minimum(large, 31)
    return np.where(d < 16, d, large)


def make_consts(inp, half):
    cv = np.zeros((128, 128), np.float32)
    p64 = np.arange(128) % 64
    for l in range(DEPTH):
        cb = CVL * l
        for i, nm in enumerate(("ffn1_norm", "mix_norm", "ffn2_norm")):
            cv[:, cb + 8 * i: cb + 8 * i + 8] = inp[nm][l].reshape(KC, 128).T
        for i, nm in enumerate(("fox_q_norm", "fox_k_norm", "swa_q_norm", "swa_k_norm")):
            cv[:, cb + 24 + i] = inp[nm][l][p64]
        cv[:, cb + 28:cb + 36] = inp["forget_bias"][l][None, :]
        cv[:, cb + 36:cb + 44] = inp["swa_sinks"][l][None, :]
    cv[:, CV_FLAG] = float(half)
    cv[:, CV_MASK] = (float(half) - 1.0) * BIG
    tab = inp["rel_bias_table"]
    kk = np.arange(128)[:, None]
    qq = np.arange(128)[None, :]
    swab = np.zeros((128, 2, 8, 128), np.float32)
    for a, d in enumerate((128 + qq - kk, qq - kk)):
        valid = (d >= 0) & (d < 128)
        g = tab[_t5_bucket(d)]
        swab[:, a] = np.where(valid[:, None, :], g.transpose(0, 2, 1), np.float32(-BIG))
    mk = np.arange(16)[:, None]
    metab = np.zeros((16, 8, 129), np.float32)
    d_first = (128 + qq) - (112 + mk) if half == 0 else np.full((16, 128), 1000)
    metab[:, :, 0:128] = tab[_t5_bucket(d_first)].transpose(0, 2, 1)
    metab[:, :, 128] = tab[_t5_bucket(np.full((16,), 1000))]
    mq = np.arange(16)[None, :]
    dq = mq - mk
    metaq = np.where((dq >= 0)[:, None, :], tab[_t5_bucket(dq)].transpose(0, 2, 1), np.float32(-BIG))
    return cv, swab.reshape(128, -1), metab.reshape(16, -1), np.ascontiguousarray(metaq.reshape(16, -1), np.float32)


def kernel(**inp):
    stage = float(os.environ.get("KSTAGE", "99"))
    inp = {k: np.asarray(v) for k, v in inp.items()}
    nc = build_program(stage)
    cmat = np.zeros((128, 384), np.float32)
    cmat[:, 0:128] = np.eye(128)
    cmat[:, 128:256] = np.triu(np.ones((128, 128)))
    cmat[:, 256:384] = np.kron(np.eye(2), np.ones((64, 64)))
    w_in = inp["w_in"].copy()
    perm = np.concatenate([np.arange(h * 64, (h + 1) * 64) for h in (0, 4, 1, 5, 2, 6, 3, 7)])
    w_in[:, :, 1544:2056] = inp["w_in"][:, :, 1544 + perm]
    w_vsf = np.ascontiguousarray(np.concatenate([inp["w_in"][:, :, 2184:2312], inp["w_in"][:, :, 1536:1544]], axis=2))
    consts = [make_consts(inp, half) for half in range(2)]
    in_maps = []
    for c in range(8):
        b, half = c // 2, c % 2
        cv, swab, metab, metaq = consts[half]
        in_maps.append({
            "x": np.ascontiguousarray(inp["x"][b, half * NOWN:(half + 1) * NOWN]),
            "meta": np.ascontiguousarray(inp["meta_tokens"]),
            "cmat": cmat, "cvec": cv, "swab": swab, "metab": metab, "metaq": metaq,
            "ffn1_w_in": inp["ffn1_w_in"], "ffn1_w_out": inp["ffn1_w_out"],
            "ffn2_w_in": inp["ffn2_w_in"], "ffn2_w_out": inp["ffn2_w_out"],
            "w_in": w_in, "w_branch_fox": inp["w_branch_fox"], "w_branch_swa": inp["w_branch_swa"],
            "w_out": inp["w_out"], "w_vsf": w_vsf,
        })
    res = run_bass_kernel_spmd(nc, in_maps, core_ids=list(range(8)))
    out = np.zeros((4, 4096, D), np.float32)
    for c in range(8):
        b, half = c // 2, c % 2
        out[b, half * NOWN:(half + 1) * NOWN] = res.results[c]["out"]
    return out
```
